# Optimizing a Trainium2 kernel written in Bass

```python
import math
import jax, jax.numpy as jnp
from jax import lax
import numpy as np

D_MODEL = 1024
BATCH = 4
SEQ = 8192
DEPTH = 2

N_A = DEPTH // 2
N_B = DEPTH - N_A
CHUNK = 128
GMLP_WIDTH = 2 * D_MODEL
GMLP_GROUPS = 8
GMLP_GROUP_DIM = GMLP_WIDTH // GMLP_GROUPS
DIFF_HEADS = 8
DIFF_HEAD_DIM = D_MODEL // (2 * DIFF_HEADS)
DIFF_V_DIM = 2 * DIFF_HEAD_DIM
D_K = DIFF_HEADS * 2 * DIFF_HEAD_DIM
D_V = DIFF_HEADS * DIFF_V_DIM
ROT_DIM = DIFF_HEAD_DIM // 4
ROPE_THETA = 500000.0
Q_BLOCK = 128
D_FF = -(-(8 * D_MODEL) // (3 * 256)) * 256
EPS = 1e-5

kernel_name = "yoco_gmlp_diffattn_hybrid"


def rmsnorm(x, g):
    x32 = x.astype(jnp.float32)
    y = x32 * lax.rsqrt(jnp.mean(x32 * x32, axis=-1, keepdims=True) + EPS)
    return (y * g.astype(jnp.float32)).astype(x.dtype)


def layernorm(x, g, b):
    x32 = x.astype(jnp.float32)
    mu = jnp.mean(x32, axis=-1, keepdims=True)
    xc = x32 - mu
    y = xc * lax.rsqrt(jnp.mean(xc * xc, axis=-1, keepdims=True) + EPS)
    return (y * g.astype(jnp.float32) + b.astype(jnp.float32)).astype(x.dtype)


def rope_partial(x, pos):
    half = ROT_DIM // 2
    inv_freq = jnp.power(ROPE_THETA, -jnp.arange(half, dtype=jnp.float32) * 2.0 / ROT_DIM)
    ang = pos.astype(jnp.float32)[:, None] * inv_freq[None, :]
    cos = jnp.cos(ang)[:, None, None, :]
    sin = jnp.sin(ang)[:, None, None, :]
    xr = x[..., :ROT_DIM].astype(jnp.float32)
    x1, x2 = xr[..., :half], xr[..., half:]
    rot = jnp.concatenate([x1 * cos - x2 * sin, x2 * cos + x1 * sin], axis=-1)
    return jnp.concatenate([rot.astype(x.dtype), x[..., ROT_DIM:]], axis=-1)


def gmlp_mixer(h, w_in, ln_g, ln_b, w_s, b_s, w_out):
    B, S, _ = h.shape
    z = jax.nn.gelu(h @ w_in, approximate=False)
    u, v = jnp.split(z, 2, axis=-1)
    v = layernorm(v, ln_g, ln_b)
    n_chunks = S // CHUNK
    v = v.reshape(B, n_chunks, CHUNK, GMLP_GROUPS, GMLP_GROUP_DIM)
    causal = jnp.tril(jnp.ones((CHUNK, CHUNK), dtype=bool))
    ws = jnp.where(causal[None], w_s, 0).astype(v.dtype)
    mixed = jnp.einsum('gts,bcsgd->bctgd', ws, v) + b_s.T.astype(v.dtype)[None, None, :, :, None]
    gated = u * mixed.reshape(B, S, GMLP_WIDTH)
    return gated @ w_out


def shared_kv(h, kv_norm_g, w_kv, pos):
    B, S, _ = h.shape
    kv = rmsnorm(h, kv_norm_g) @ w_kv
    k, v = jnp.split(kv, [D_K], axis=-1)
    k = rope_partial(k.reshape(B, S, DIFF_HEADS, 2, DIFF_HEAD_DIM), pos)
    v = v.reshape(B, S, DIFF_HEADS, DIFF_V_DIM)
    return k, v


def diff_attention(h, k, v, pos, w_q, lam_q, lam_k, sub_g, w_o, lam_init):
    B, S, _ = h.shape
    q = rope_partial((h @ w_q).reshape(B, S, DIFF_HEADS, 2, DIFF_HEAD_DIM), pos)
    lq = lam_q.astype(jnp.float32)
    lk = lam_k.astype(jnp.float32)
    lam = jnp.exp(jnp.sum(lq[0] * lk[0])) - jnp.exp(jnp.sum(lq[1] * lk[1])) + lam_init
    n_blocks = S // Q_BLOCK
    qb = q.reshape(B, n_blocks, Q_BLOCK, DIFF_HEADS, 2, DIFF_HEAD_DIM).transpose(1, 0, 2, 3, 4, 5)
    k_pos = jnp.arange(S, dtype=jnp.int32)
    scale = DIFF_HEAD_DIM ** -0.5

    def block(args):
        q_blk, i = args
        q_pos = i * Q_BLOCK + jnp.arange(Q_BLOCK, dtype=jnp.int32)
        s = jnp.einsum('bqhcd,bkhcd->bhcqk', q_blk, k).astype(jnp.float32) * scale
        s = jnp.where(k_pos[None, :] <= q_pos[:, None], s, -jnp.inf)
        p = jax.nn.softmax(s, axis=-1)
        a = p[:, :, 0] - lam * p[:, :, 1]
        return jnp.einsum('bhqk,bkhe->bqhe', a.astype(v.dtype), v)

    o = lax.map(block, (qb, jnp.arange(n_blocks, dtype=jnp.int32)))
    o = o.transpose(1, 0, 2, 3, 4).reshape(B, S, DIFF_HEADS, DIFF_V_DIM)
    o = rmsnorm(o, sub_g) * (1.0 - lam_init)
    return o.reshape(B, S, D_V) @ w_o


def swiglu(h, w_gu, w_down):
    gate, up = jnp.split(h @ w_gu, 2, axis=-1)
    return (jax.nn.silu(gate) * up) @ w_down


def setup_inputs(seed: int = 0) -> dict:
    key = jax.random.key(seed)
    ks = jax.random.split(key, 20)
    f32 = jnp.float32
    nrm = lambda k, shape, s: jax.random.normal(k, shape, f32) * s
    return {
        "x": nrm(ks[0], (BATCH, SEQ, D_MODEL), 1.0),
        "attn_norm_g": 1.0 + nrm(ks[1], (DEPTH, D_MODEL), 0.02),
        "ffn_norm_g": 1.0 + nrm(ks[2], (DEPTH, D_MODEL), 0.02),
        "gmlp_w_in": nrm(ks[3], (N_A, D_MODEL, 2 * GMLP_WIDTH), D_MODEL ** -0.5),
        "gmlp_ln_g": 1.0 + nrm(ks[4], (N_A, GMLP_WIDTH), 0.02),
        "gmlp_ln_b": nrm(ks[5], (N_A, GMLP_WIDTH), 0.02),
        "gmlp_w_s": nrm(ks[6], (N_A, GMLP_GROUPS, CHUNK, CHUNK), CHUNK ** -0.5),
        "gmlp_b_s": 1.0 + nrm(ks[7], (N_A, GMLP_GROUPS, CHUNK), 0.1),
        "gmlp_w_out": nrm(ks[8], (N_A, GMLP_WIDTH, D_MODEL), GMLP_WIDTH ** -0.5),
        "kv_norm_g": 1.0 + nrm(ks[9], (D_MODEL,), 0.02),
        "w_kv": nrm(ks[10], (D_MODEL, D_K + D_V), D_MODEL ** -0.5),
        "diff_w_q": nrm(ks[11], (N_B, D_MODEL, D_K), D_MODEL ** -0.5),
        "diff_lambda_q": nrm(ks[12], (N_B, 2, DIFF_HEAD_DIM), 0.1),
        "diff_lambda_k": nrm(ks[13], (N_B, 2, DIFF_HEAD_DIM), 0.1),
        "diff_sub_g": 1.0 + nrm(ks[14], (N_B, DIFF_V_DIM), 0.02),
        "diff_w_o": nrm(ks[15], (N_B, D_V, D_MODEL), D_V ** -0.5),
        "ffn_w_gu": nrm(ks[16], (DEPTH, D_MODEL, 2 * D_FF), D_MODEL ** -0.5),
        "ffn_w_down": nrm(ks[17], (DEPTH, D_FF, D_MODEL), D_FF ** -0.5),
        "final_norm_g": 1.0 + nrm(ks[18], (D_MODEL,), 0.02),
    }


def reference(x, attn_norm_g, ffn_norm_g, gmlp_w_in, gmlp_ln_g, gmlp_ln_b, gmlp_w_s, gmlp_b_s,
              gmlp_w_out, kv_norm_g, w_kv, diff_w_q, diff_lambda_q, diff_lambda_k, diff_sub_g,
              diff_w_o, ffn_w_gu, ffn_w_down, final_norm_g):
    S = x.shape[1]
    pos = jnp.arange(S, dtype=jnp.int32)
    h = x
    k_sh, v_sh = None, None
    for l in range(DEPTH):
        hn = rmsnorm(h, attn_norm_g[l])
        if l < N_A:
            a = l
            h = h + gmlp_mixer(hn, gmlp_w_in[a], gmlp_ln_g[a], gmlp_ln_b[a], gmlp_w_s[a],
                               gmlp_b_s[a], gmlp_w_out[a])
        else:
            b = l - N_A
            lam_init = 0.8 - 0.6 * math.exp(-0.3 * l)
            h = h + diff_attention(hn, k_sh, v_sh, pos, diff_w_q[b], diff_lambda_q[b],
                                   diff_lambda_k[b], diff_sub_g[b], diff_w_o[b], lam_init)
        h = h + swiglu(rmsnorm(h, ffn_norm_g[l]), ffn_w_gu[l], ffn_w_down[l])
        if l == N_A - 1:
            k_sh, v_sh = shared_kv(h, kv_norm_g, w_kv, pos)
    return rmsnorm(h, final_norm_g)
```

```python
import contextlib
import math
import numpy as np
import ml_dtypes
import concourse.bass as bass
import concourse.mybir as mybir
from concourse.bass_utils import run_bass_kernel_spmd

F32 = mybir.dt.float32
BF16 = mybir.dt.bfloat16
AF = mybir.ActivationFunctionType
ALU = mybir.AluOpType
AX = mybir.AxisListType

D = 1024
SEQ = 8192
NT = 8
TT = 512
DFF = 2816
EPS = 1e-5
SB_OWN = ([0, 3, 4, 7, 8, 11, 12, 15], [1, 2, 5, 6, 9, 10, 13, 14])
LAM_INIT = 0.8 - 0.6 * math.exp(-0.3 * 1)
ENGS = ("pe", "act", "dve", "pool", "sp")


class Op:
    __slots__ = ("eng", "emit", "deps", "signal", "cnt", "dma_sem", "dma_cnt", "idx")


class Prog:
    def __init__(self, nc):
        self.nc = nc
        self.ops = []
        self.by_eng = {e: [] for e in ENGS}
        self.last_w = {}
        self.readers = {}
        self.dma_sem_count = {}
        self.final_waits = []

    def op(self, eng, emit, reads=(), writes=(), dma_sem=None, fresh=()):
        for t in fresh:
            assert self.last_w.get(t) is None or self.readers.get(t), ("overwrite of unread", t)
        o = Op()
        o.eng = eng
        o.emit = emit
        o.signal = False
        o.cnt = 0
        o.dma_sem = dma_sem
        o.dma_cnt = 0
        o.idx = len(self.ops)
        deps = set()
        for r in reads:
            w = self.last_w.get(r)
            if w is not None:
                deps.add(w)
        for t in writes:
            w = self.last_w.get(t)
            if w is not None:
                deps.add(w)
            for rd in self.readers.get(t, ()):
                deps.add(rd)
        o.deps = deps
        if dma_sem is not None:
            c = self.dma_sem_count.get(dma_sem, 0) + 1
            self.dma_sem_count[dma_sem] = c
            o.dma_cnt = c
        for r in reads:
            self.readers.setdefault(r, []).append(o.idx)
        for t in writes:
            self.last_w[t] = o.idx
            self.readers[t] = []
        self.ops.append(o)
        self.by_eng[eng].append(o)
        return o

    def barrier(self):
        deps = set()
        seen_dma = {}
        for o in self.ops:
            if o.dma_sem is not None:
                seen_dma[o.dma_sem] = o.idx
        deps.update(seen_dma.values())
        for e in ENGS:
            for o in reversed(self.by_eng[e]):
                if o.dma_sem is None:
                    deps.add(o.idx)
                    break
        for e in ENGS:
            o = self.op(e, lambda eng: eng.nop())
            o.deps |= deps

    def emit_all(self, es):
        nc = self.nc
        ops = self.ops
        sems = {e: es.enter_context(nc.semaphore("s_" + e)) for e in ENGS}
        dma_sems = {}
        for i, k in enumerate(self.dma_sem_count):
            dma_sems[k] = es.enter_context(nc.semaphore("d%d" % i))
        for o in ops:
            for d in o.deps:
                p = ops[d]
                if p.dma_sem is None:
                    if p.eng == "pe" and o.eng == "pe":
                        continue
                    p.signal = True
        for e in ENGS:
            c = 0
            for o in self.by_eng[e]:
                if o.dma_sem is None and o.signal:
                    c += 1
                    o.cnt = c
        prog = self

        def run(e, eng):
            waited = {}
            for o in prog.by_eng[e]:
                need = {}
                for d in o.deps:
                    p = ops[d]
                    if p.dma_sem is not None:
                        key = ("d", p.dma_sem)
                        val = 16 * p.dma_cnt
                    else:
                        if p.eng == "pe" and e == "pe":
                            continue
                        key = ("e", p.eng)
                        val = p.cnt
                    if val > need.get(key, 0):
                        need[key] = val
                if o.dma_sem is not None and o.dma_cnt > 1:
                    key = ("d", o.dma_sem)
                    need[key] = max(need.get(key, 0), 16 * (o.dma_cnt - 1))
                for key, val in need.items():
                    if waited.get(key, 0) >= val:
                        continue
                    waited[key] = val
                    sem = dma_sems[key[1]] if key[0] == "d" else sems[key[1]]
                    eng.wait_ge(sem, val)
                ins = o.emit(eng)
                if o.dma_sem is not None:
                    ins.then_inc(dma_sems[o.dma_sem], 16)
                elif o.signal:
                    ins.then_inc(sems[e], 1)
            for (e2, key) in prog.final_waits:
                if e2 == e:
                    eng.wait_ge(dma_sems[key], 16 * prog.dma_sem_count[key])

        with nc.Block() as block:
            @block.tensor
            def _(eng):
                run("pe", eng)

            @block.scalar
            def _(eng):
                run("act", eng)

            @block.vector
            def _(eng):
                run("dve", eng)

            @block.gpsimd
            def _(eng):
                run("pool", eng)

            @block.sync
            def _(eng):
                run("sp", eng)


class Arena:
    def __init__(self, nc, es, nbytes):
        self.t = es.enter_context(nc.sbuf_tensor("arena", [128, nbytes // 4], F32))
        self.off = 0
        self.cap = nbytes // 4

    def carve(self, shape, dtype):
        n = int(np.prod(shape))
        nb = n * (4 if dtype == F32 else 2)
        w = (nb + 31) // 32 * 8
        assert self.off + w <= self.cap, ("arena overflow", self.off * 4, w * 4, self.cap * 4)
        ap = self.t[:, self.off:self.off + w]
        self.off += w
        if dtype != F32:
            ap = ap.bitcast(dtype)
        ap = ap[:, 0:n]
        if len(shape) == 2:
            ap = ap.rearrange("p (a b) -> p a b", a=shape[0])
        elif len(shape) == 3:
            ap = ap.rearrange("p (a b c) -> p a b c", a=shape[0], b=shape[1])
        return ap


class Ctx:
    pass


def _common(nc, es, P, arena_bytes):
    C = Ctx()
    C.nc, C.P = nc, P
    C.A = Arena(nc, es, arena_bytes)
    C.ps2 = [es.enter_context(nc.psum_tensor("ps%d" % i, [128, 1024], F32)) for i in range(4)]
    C.ps = [C.ps2[i // 2][:, (i % 2) * 512:(i % 2 + 1) * 512] for i in range(8)]
    C.bank_i = 0
    A = C.A
    C.h = A.carve((8, 512), F32)
    C.xn = A.carve((8, 512), BF16)
    C.sq = A.carve((8, 512), BF16)
    C.tpre = A.carve((512,), F32)
    C.rstd = A.carve((512,), F32)
    C.neghalf = A.carve((512,), F32)
    C.ones = A.carve((128,), BF16)
    C.hmid = A.carve((22, 512), BF16)
    C.yout = C.hmid[:, 0:16, :].rearrange("p a b -> p (a b)").bitcast(F32).rearrange("p (a b) -> p a b", a=8)
    C.sg = [A.carve((512,), F32) for _ in range(2)]
    C.cos = A.carve((512,), F32)
    C.sin = A.carve((512,), F32)
    C.t1 = A.carve((512,), F32)
    C.t2 = A.carve((512,), F32)
    C.khi = [A.carve((512,), BF16) for _ in range(2)]
    C.klo = [A.carve((512,), BF16) for _ in range(2)]
    C.Rm = A.carve((128,), BF16)
    C.hl_i = 0
    C.nslots = 4
    C.slots = [A.carve((4096,), BF16) for _ in range(C.nslots)]
    C.slot_i = 0
    P.op("pool", lambda e: e.memset(C.neghalf, -0.5), writes=["neghalf"])
    P.op("pool", lambda e: e.memset(C.ones, 1.0), writes=["ones"])
    return C


def nb(C):
    b = C.bank_i % 8
    C.bank_i += 1
    return b


def cast_weight(C, name, src, dst, rows, cols, cstep=512, rstep=1024):
    P = C.P
    starts = list(range(0, cols, cstep))
    if name == "win":
        starts = starts[4:] + starts[:4]
    for c0 in starts:
        c1 = min(cols, c0 + cstep)
        for r in range(0, rows, rstep):
            r1 = min(rows, r + rstep)
            P.op("pool", lambda e, r=r, r1=r1, c0=c0, c1=c1: e.dma_start(out=dst[r:r1, c0:c1], in_=src[r:r1, c0:c1]),
                 writes=[("wbf", name, c0 // cstep, r // rstep)], dma_sem=("wc", name, c0 // cstep, r // rstep))


def load_piece(C, name, w, r0, nk, c0, ncols):
    P = C.P
    s = C.slot_i % C.nslots
    C.slot_i += 1
    view = C.slots[s][:, 0:nk * ncols].rearrange("p (k n) -> p k n", k=nk)
    src = w[r0:r0 + nk * 128, c0:c0 + ncols].rearrange("(k p) n -> p k n", p=128)
    rd = [("wbf", name, cb, rb) for cb in range(c0 // 512, (c0 + ncols - 1) // 512 + 1)
          for rb in range(r0 // 1024, (r0 + nk * 128 - 1) // 1024 + 1)]
    P.op("sp", lambda e: e.dma_start(out=view, in_=src), reads=rd,
         writes=[("slot", s)], dma_sem=("slot", s))
    return view, ("slot", s)


def mm_A(C, b, piece, ptok, coff, nk, rhs_fn, rhs_toks, start=True, stop=True):
    P = C.P
    for k in range(nk):
        P.op("pe", lambda e, k=k: e.matmul(C.ps[b][:], lhsT=piece[:, k, coff:coff + 128], rhs=rhs_fn(k),
                                            start=(start and k == 0), stop=(stop and k == nk - 1)),
             reads=[ptok, rhs_toks(k)], writes=[("ps", b)], fresh=([("ps", b)] if (start and k == 0) else []))


def mm_B(C, b, piece, ptok, tc, ncols=512):
    P = C.P
    for k in range(8):
        P.op("pe", lambda e, k=k: e.matmul(C.ps[b][:, 0:ncols], lhsT=C.xn[:, k, tc * 128:(tc + 1) * 128],
                                            rhs=piece[:, k, 0:ncols], start=(k == 0), stop=(k == 7)),
             reads=[ptok, ("xn", k)], writes=[("ps", b)])


def rsqrt(C, out, in_, otok, itok):
    P = C.P
    P.op("act", lambda e: e.activation(out=in_, in_=in_, func=AF.Ln), reads=[], writes=[itok])
    P.op("act", lambda e: e.activation(out=out, in_=in_, func=AF.Exp, scale=-0.5), reads=[itok], writes=[otok])


def rmsnorm(C, gcol, out_f32=False, hbuf=None, htoks=None):
    P = C.P
    hb = C.h if hbuf is None else hbuf
    if htoks is None:
        htoks = lambda c: [("h", c)]
    for c in range(8):
        P.op("act", lambda e, c=c: e.activation(out=C.sq[:, c, :], in_=hb[:, c, :], func=AF.Square),
             reads=htoks(c), writes=[("sq", c)])
    b = nb(C)
    for c in range(8):
        P.op("pe", lambda e, c=c: e.matmul(C.ps[b][:], lhsT=C.ones, rhs=C.sq[:, c, :], start=(c == 0), stop=(c == 7)),
             reads=[("sq", c), "ones"], writes=[("ps", b)])
    P.op("dve", lambda e: e.tensor_scalar(out=C.tpre, in0=C.ps[b][:], scalar1=1.0 / D, scalar2=EPS,
                                          op0=ALU.mult, op1=ALU.add), reads=[("ps", b)], writes=["tpre"])
    rsqrt(C, C.rstd, C.tpre, "rstd", "tpre")
    for c in range(8):
        if out_f32:
            P.op("dve", lambda e, c=c: e.scalar_tensor_tensor(out=C.yout[:, c, :], in0=hb[:, c, :], scalar=gcol[:, c:c + 1],
                                                              in1=C.rstd, op0=ALU.mult, op1=ALU.mult),
                 reads=htoks(c) + ["rstd", "vecs"], writes=[("hmid", 2 * c), ("hmid", 2 * c + 1)])
        else:
            P.op("dve", lambda e, c=c: e.scalar_tensor_tensor(out=C.xn[:, c, :], in0=hb[:, c, :], scalar=gcol[:, c:c + 1],
                                                              in1=C.rstd, op0=ALU.mult, op1=ALU.mult),
                 reads=htoks(c) + ["rstd", "vecs"], writes=[("xn", c)])


def ffn(C, wgu, wdn, lname):
    ffn_up(C, wgu, lname)
    ffn_down(C, wdn, lname)


def ffn_up(C, wgu, lname):
    P = C.P
    f = 0
    gi = 0
    for g0 in range(0, DFF, 512):
        ncols = min(512, DFF - g0)
        pg, tg = load_piece(C, "wgu" + lname, wgu, 0, 8, g0, ncols)
        pu, tu = load_piece(C, "wgu" + lname, wgu, 0, 8, DFF + g0, ncols)
        for j in range(ncols // 128):
            bg = nb(C)
            mm_A(C, bg, pg, tg, j * 128, 8, lambda k: C.xn[:, k, :], lambda k: ("xn", k))
            bu = nb(C)
            mm_A(C, bu, pu, tu, j * 128, 8, lambda k: C.xn[:, k, :], lambda k: ("xn", k))
            s = C.sg[gi % 2]
            stok = ("sg", gi % 2)
            gi += 1
            P.op("act", lambda e, bg=bg, s=s: e.activation(out=s, in_=C.ps[bg][:], func=AF.Silu),
                 reads=[("ps", bg)], writes=[stok])
            P.op("dve", lambda e, bu=bu, s=s, f=f: e.tensor_tensor(out=C.hmid[:, f, :], in0=s, in1=C.ps[bu][:], op=ALU.mult),
                 reads=[("ps", bu), stok], writes=[("hmid", f)])
            f += 1
    assert f == 22


def ffn_down(C, wdn, lname):
    P = C.P
    for cb in range(4):
        banks = [nb(C), nb(C)]
        for kh in range(2):
            pd, td = load_piece(C, "wdn" + lname, wdn, kh * 11 * 128, 11, cb * 256, 256)
            for j in range(2):
                mm_A(C, banks[j], pd, td, j * 128, 11, lambda k, kh=kh: C.hmid[:, kh * 11 + k, :],
                     lambda k, kh=kh: ("hmid", kh * 11 + k), start=(kh == 0), stop=(kh == 1))
        for j in range(2):
            m = cb * 2 + j
            P.op("dve", lambda e, m=m, b=banks[j]: e.tensor_tensor(out=C.h[:, m, :], in0=C.h[:, m, :], in1=C.ps[b][:], op=ALU.add),
                 reads=[("ps", banks[j])], writes=[("h", m)])


def rope_heads(C, items, out_fn, add_eng="pool"):
    P = C.P
    prev = None

    def finish(b1, hd, hl):
        b2 = nb(C)
        P.op("pe", lambda e: e.matmul(C.ps[b2][:], lhsT=C.Rm, rhs=C.khi[hl], start=True, stop=False),
             reads=[("khi", hl), "Rm"], writes=[("ps", b2)], fresh=[("ps", b2)])
        P.op("pe", lambda e: e.matmul(C.ps[b2][:], lhsT=C.Rm, rhs=C.klo[hl], start=False, stop=True),
             reads=[("klo", hl), "Rm"], writes=[("ps", b2)])
        ap, tok = out_fn(hd)
        rope_evac(C, b1, b2, ap, tok, add_eng)

    for piece, ptok, coff, hd in items:
        b1 = nb(C)
        mm_A(C, b1, piece, ptok, coff, 8, lambda k: C.xn[:, k, :], lambda k: ("xn", k))
        hl = C.hl_i % 2
        C.hl_i += 1
        P.op("act", lambda e, b1=b1, hl=hl: e.copy(out=C.khi[hl], in_=C.ps[b1][:]), reads=[("ps", b1)], writes=[("khi", hl)])
        P.op("dve", lambda e, b1=b1, hl=hl: e.tensor_tensor(out=C.klo[hl], in0=C.ps[b1][:], in1=C.khi[hl], op=ALU.subtract),
             reads=[("ps", b1), ("khi", hl)], writes=[("klo", hl)])
        if prev is not None:
            finish(*prev)
        prev = (b1, hd, hl)
    finish(*prev)


def rope_evac(C, b1, b2, out_ap, out_tok, add_eng="pool"):
    P = C.P
    P.op("dve", lambda e: e.tensor_tensor(out=C.t1, in0=C.ps[b1][:], in1=C.cos, op=ALU.mult),
         reads=[("ps", b1), "cos"], writes=["t1"])
    P.op("dve", lambda e: e.tensor_tensor(out=C.t2, in0=C.ps[b2][:], in1=C.sin, op=ALU.mult),
         reads=[("ps", b2), "sin"], writes=["t2"])
    P.op(add_eng, lambda e: e.tensor_tensor(out=out_ap, in0=C.t1, in1=C.t2, op=ALU.add),
         reads=["t1", "t2"], writes=[out_tok])


def load_rope(C, cosT, sinT, t):
    P = C.P
    P.op("sp", lambda e: e.dma_start(out=C.cos, in_=cosT[:, t * TT:(t + 1) * TT]), writes=["cos"], dma_sem="cos")
    P.op("sp", lambda e: e.dma_start(out=C.sin, in_=sinT[:, t * TT:(t + 1) * TT]), writes=["sin"], dma_sem="sin")


def dram_in(nc, name, shape, dt=F32):
    return nc.dram_tensor(name, list(shape), dt, kind="ExternalInput").ap()


def emit_l0(C, io):
    P, nc, A = C.P, C.nc, C.A
    u = A.carve((16, 512), BF16)
    v = A.carve((4, 2048), F32)
    vh = A.carve((4, 2048), BF16)
    st = A.carve((4, 24), F32)
    mv = A.carve((4, 2), F32)
    rs = A.carve((4, 1), F32)
    tmpg = [A.carve((512,), F32) for _ in range(2)]
    ksb = A.carve((8, 512), BF16)
    vsb = A.carve((8, 4, 128), BF16)
    vecs = A.carve((56,), F32)
    wsb = A.carve((8, 128), BF16)
    wsf = v[:, 0, 0:1024].rearrange("p (g t) -> p g t", g=8)
    bsb = v[:, 0, 1024:2048].rearrange("p (g t) -> p g t", g=8)
    tri = A.carve((128,), F32)
    Bc = A.carve((16, 128), F32)

    P.op("sp", lambda e: e.dma_start(out=vecs, in_=io["vecs0"]), writes=["vecs"], dma_sem="c0")
    P.op("sp", lambda e: e.dma_start(out=wsf, in_=io["wsT"].rearrange("p (g t) -> p g t", g=8)), writes=["wsf"], dma_sem="c0")
    P.op("sp", lambda e: e.dma_start(out=bsb, in_=io["bsb"].rearrange("p (g t) -> p g t", g=8)), writes=["bsb"], dma_sem="c0")
    P.op("sp", lambda e: e.dma_start(out=tri, in_=io["tri"]), writes=["tri"], dma_sem="c0")
    P.op("dve", lambda e: e.tensor_tensor(out=wsb, in0=wsf, in1=tri.unsqueeze(1).to_broadcast([128, 8, 128]), op=ALU.mult),
         reads=["wsf", "tri", ("v", 0, 0), ("v", 0, 1)], writes=["wsb"])
    for half in range(2):
        b = nb(C)
        P.op("pe", lambda e, b=b, half=half: e.matmul(C.ps[b][:], lhsT=C.ones, rhs=wsb[:, half * 4:(half + 1) * 4, :],
                                                      start=True, stop=True), reads=["wsb", "ones"], writes=[("ps", b)])
        for gg in range(4):
            g = half * 4 + gg
            for cc in range(2):
                c = g * 2 + cc
                P.op("dve", lambda e, b=b, gg=gg, g=g, c=c: e.scalar_tensor_tensor(
                    out=Bc[:, c, :], in0=C.ps[b][:, gg * 128:(gg + 1) * 128], scalar=vecs[:, 40 + c:41 + c],
                    in1=bsb[:, g, :], op0=ALU.mult, op1=ALU.add), reads=[("ps", b), "vecs", "bsb", ("v", 0, 2), ("v", 0, 3)], writes=[("Bc", c)])

    def tile0(t):
        o_, i_ = t // NT, t % NT
        tsl = slice(t * TT, (t + 1) * TT)
        isl = slice(i_ * TT, (i_ + 1) * TT)
        P.op("sp", lambda e, tsl=tsl: e.dma_start(out=C.h, in_=io["xT"][:, tsl].rearrange("(c p) n -> p c n", p=128)),
             writes=[("h", c) for c in range(8)], dma_sem="h")
        load_rope(C, io["cosT"], io["sinT"], t)
        rmsnorm(C, vecs[:, 0:8])
        for nbk in range(4):
            pw, tw = load_piece(C, "win", io["win_bf"], 0, 8, 2048 + nbk * 512, 512)
            for tc in range(4):
                b = nb(C)
                mm_B(C, b, pw, tw, tc)
                P.op("act", lambda e, b=b, tc=tc, nbk=nbk: e.activation(out=v[:, tc, nbk * 512:(nbk + 1) * 512], in_=C.ps[b][:], func=AF.Gelu),
                     reads=[("ps", b)], writes=[("v", tc, nbk)])
        for tc in range(4):
            for nbk in range(4):
                P.op("dve", lambda e, tc=tc, nbk=nbk: e.bn_stats(out=st[:, tc, nbk * 6:(nbk + 1) * 6], in_=v[:, tc, nbk * 512:(nbk + 1) * 512]),
                     reads=[("v", tc, nbk)], writes=[("st", tc, nbk)])
            P.op("dve", lambda e, tc=tc: e.bn_aggr(out=mv[:, tc, :], in_=st[:, tc, :]),
                 reads=[("st", tc, n_) for n_ in range(4)], writes=[("mv", tc)])
            P.op("dve", lambda e, tc=tc: e.tensor_scalar(out=rs[:, tc, :], in0=mv[:, tc, 1:2], scalar1=EPS, scalar2=None, op0=ALU.add),
                 reads=[("mv", tc)], writes=[("rs0", tc)])
            if t == 0:
                P.op("act", lambda e, tc=tc: e.activation(out=rs[:, tc, :], in_=rs[:, tc, :], func=AF.Ln), reads=[], writes=[("rs0", tc)])
                P.op("act", lambda e, tc=tc: e.activation(out=rs[:, tc, :], in_=rs[:, tc, :], func=AF.Exp, scale=-0.5), reads=[("rs0", tc)], writes=[("rs", tc)])
            else:
                P.op("pool", lambda e, tc=tc: e.tensor_tensor(out=rs[:, tc, :], in0=rs[:, tc, :], in1=C.neghalf[:, 0:1], op=ALU.pow),
                     reads=[("rs0", tc), "neghalf"], writes=[("rs", tc)])
            P.op("dve", lambda e, tc=tc: e.tensor_scalar(out=vh[:, tc, :], in0=v[:, tc, :], scalar1=mv[:, tc, 0:1], scalar2=rs[:, tc, :],
                                                         op0=ALU.subtract, op1=ALU.mult),
                 reads=[("v", tc, n_) for n_ in range(4)] + [("mv", tc), ("rs", tc)], writes=[("vh", tc)])
        for pc in range(4):
            pw, tw = load_piece(C, "win", io["win_bf"], 0, 8, pc * 512, 512)
            for j in range(4):
                c = pc * 4 + j
                b = nb(C)
                mm_A(C, b, pw, tw, j * 128, 8, lambda k: C.xn[:, k, :], lambda k: ("xn", k))
                P.op("act", lambda e, b=b, c=c: e.activation(out=u[:, c, :], in_=C.ps[b][:], func=AF.Gelu),
                     reads=[("ps", b)], writes=[("u", c)])
        for c in range(16):
            b = nb(C)
            for tc in range(4):
                P.op("pe", lambda e, b=b, c=c, tc=tc: e.matmul(C.ps[b][:, tc * 128:(tc + 1) * 128], lhsT=vh[:, tc, c * 128:(c + 1) * 128],
                                                               rhs=wsb[:, c // 2, :], start=True, stop=True),
                     reads=[("vh", tc), "wsb"], writes=[("ps", b)])
            tg = tmpg[c % 2]
            P.op("dve", lambda e, b=b, c=c, tg=tg: e.scalar_tensor_tensor(
                out=tg.rearrange("p (a t) -> p a t", a=4), in0=C.ps[b][:].rearrange("p (a t) -> p a t", a=4),
                scalar=vecs[:, 24 + c:25 + c], in1=Bc[:, c, :].unsqueeze(1).to_broadcast([128, 4, 128]),
                op0=ALU.mult, op1=ALU.add), reads=[("ps", b), ("Bc", c), "vecs"], writes=[("tmpg", c % 2)])
            P.op("dve" if t == 0 else "pool", lambda e, c=c, tg=tg: e.tensor_tensor(out=u[:, c, :], in0=tg, in1=u[:, c, :], op=ALU.mult),
                 reads=[("tmpg", c % 2)], writes=[("u", c)])
        for cb in range(2):
            banks = [nb(C) for _ in range(4)]
            for kh in range(2):
                pw, tw = load_piece(C, "wout", io["wout_bf"], kh * 1024, 8, cb * 512, 512)
                for j in range(4):
                    mm_A(C, banks[j], pw, tw, j * 128, 8, lambda k, kh=kh: u[:, kh * 8 + k, :], lambda k, kh=kh: ("u", kh * 8 + k),
                         start=(kh == 0), stop=(kh == 1))
            for j in range(4):
                m = cb * 4 + j
                P.op("dve", lambda e, m=m, b=banks[j]: e.tensor_tensor(out=C.h[:, m, :], in0=C.h[:, m, :], in1=C.ps[b][:], op=ALU.add),
                     reads=[("ps", banks[j])], writes=[("h", m)])
        rmsnorm(C, vecs[:, 8:16])
        ffn(C, io["wgu0_bf"], io["wdn0_bf"], "0")
        if o_ == 0:
            P.op("pool", lambda e, isl=isl: e.dma_start(out=io["h1T"][:, isl].rearrange("(c p) n -> p c n", p=128), in_=C.h),
                 reads=[("h", c) for c in range(8)], writes=[("h1s", i_)], dma_sem="h1o")
        rmsnorm(C, vecs[:, 16:24])
        items = []
        for hg in range(2):
            pk, tk = load_piece(C, "wkvx", io["wkvx_bf"], 0, 8, hg * 512, 512)
            items += [(pk, tk, j * 128, hg * 4 + j) for j in range(4)]
        rope_heads(C, items, lambda hd: (ksb[:, hd, :], ("ksb", hd)), add_eng=("dve" if t == 0 else "pool"))
        P.op("pool", lambda e, isl=isl, o_=o_: e.dma_start(out=io["Kd"][o_ * 1024:(o_ + 1) * 1024, isl].rearrange("(c p) n -> p c n", p=128), in_=ksb),
             reads=[("ksb", hd) for hd in range(8)], writes=[("Kd", o_, i_)], dma_sem="ko")
        for vb in range(2):
            pv, tv = load_piece(C, "wkvx", io["wkvx_bf"], 0, 8, 1024 + vb * 512, 512)
            for tc in range(4):
                b = nb(C)
                mm_B(C, b, pv, tv, tc)
                P.op("act", lambda e, b=b, vb=vb, tc=tc: e.copy(out=vsb[:, vb * 4:(vb + 1) * 4, tc, :],
                                                                in_=C.ps[b][:].rearrange("p (h e) -> p h e", h=4)),
                     reads=[("ps", b)], writes=[("vsb", vb, tc)])
        vd4 = io["Vd"][o_ * 128:(o_ + 1) * 128, :].rearrange("p (h b e) -> p h b e", h=8, b=32)
        P.op("pool", lambda e, i_=i_, vd4=vd4: e.dma_start(out=vd4[:, :, i_ * 4:(i_ + 1) * 4, :], in_=vsb),
             reads=[("vsb", vb, tc) for vb in range(2) for tc in range(4)], writes=[("Vd", o_, i_)], dma_sem="vo")

    for t in range(2 * NT):
        tile0(t)


def emit_l1(C, io):
    P, nc, A = C.P, C.nc, C.A
    q = A.carve((8, 512), BF16)
    on = A.carve((8, 512), BF16)
    NKB = 3
    k0_off = A.off
    kbuf = [A.carve((4096,), BF16) for _ in range(NKB)]
    vbuf = [A.carve((32, 128), BF16) for _ in range(NKB)]
    NPB = 3
    pb = [A.carve((2, 512), BF16) for _ in range(NPB)]
    acc = [A.carve((512,), F32) for _ in range(2)]
    accbf = A.carve((512,), BF16)
    dmask = A.carve((4, 512), BF16)
    vis = A.carve((2,), F32)
    r0 = A.carve((512,), F32)
    r1 = A.carve((512,), F32)
    o0 = A.carve((512,), F32)
    o1 = A.carve((512,), F32)
    of = A.carve((512,), F32)
    osq = A.carve((512,), BF16)
    vecs = A.carve((32,), F32)
    lqk = A.carve((256,), F32)
    lsm = A.carve((8,), F32)

    P.op("sp", lambda e: e.dma_start(out=vecs[:, 0:25], in_=io["vecs1"]), writes=["vecs"], dma_sem="c0")
    P.op("sp", lambda e: e.dma_start(out=lqk, in_=io["lqk"]), writes=["lqk"], dma_sem="c0")
    P.op("sp", lambda e: e.dma_start(out=dmask, in_=io["dmask"].rearrange("p (b q) -> p b q", b=4)), writes=["dmask"], dma_sem="c0")
    P.op("sp", lambda e: e.dma_start(out=vis, in_=io["vis"]), writes=["vis"], dma_sem="c0")
    P.op("dve", lambda e: e.tensor_tensor(out=lqk[:, 0:128], in0=lqk[:, 0:128], in1=lqk[:, 128:256], op=ALU.mult),
         reads=["lqk"], writes=["lqk2"])
    P.op("dve", lambda e: e.tensor_reduce(out=lsm[:, 0:2], in_=lqk[:, 0:128].rearrange("p (a d) -> p a d", a=2), axis=AX.X, op=ALU.add),
         reads=["lqk2"], writes=["ls0"])
    P.op("act", lambda e: e.activation(out=lsm[:, 2:4], in_=lsm[:, 0:2], func=AF.Exp), reads=["ls0"], writes=["ls1"])
    P.op("dve", lambda e: e.tensor_tensor(out=lsm[:, 4:5], in0=lsm[:, 3:4], in1=lsm[:, 2:3], op=ALU.subtract), reads=["ls1"], writes=["ls2"])
    P.op("dve", lambda e: e.tensor_scalar(out=lsm[:, 5:6], in0=lsm[:, 4:5], scalar1=-LAM_INIT, scalar2=None, op0=ALU.add), reads=["ls2"], writes=["neglam"])
    P.op("dve", lambda e: e.tensor_scalar(out=lsm[:, 6:7], in0=vecs[:, 24:25], scalar1=1.0 - LAM_INIT, scalar2=None, op0=ALU.mult), reads=["vecs"], writes=["subg"])
    neglam = lsm[:, 5:6]
    subg = lsm[:, 6:7]

    kg = io["Kd"]
    vg = io["Vd"].rearrange("r (h b e) -> r h b e", h=8, b=32)
    st8 = {"kvi": 0, "pbi": 0, "head": 0}

    hq = A.t[:, k0_off:k0_off + 4096]
    hq = hq.rearrange("p (a b) -> p a b", a=8)
    hq_toks = lambda c: [("kbuf", 0), ("kbuf", 1)]

    def prologue_load(t):
        tsl = slice(t * TT, (t + 1) * TT)
        P.op("sp", lambda e: e.dma_start(out=hq, in_=io["h1T"][:, tsl].rearrange("(c p) n -> p c n", p=128)),
             reads=[("h1s", t)], writes=[("kbuf", 0), ("kbuf", 1)], dma_sem="hq")
        load_rope(C, io["cosT"], io["sinT"], t)

    def prologue_norm(t):
        rmsnorm(C, vecs[:, 0:8], hbuf=hq, htoks=hq_toks)

    def prologue_q(t):
        items = []
        for hg in range(2):
            pq, tq = load_piece(C, "wqx", io["wqx_bf"], 0, 8, hg * 512, 512)
            items += [(pq, tq, j * 128, hg * 4 + j) for j in range(4)]
        rope_heads(C, items, lambda hd: (q[:, hd, :], ("q", hd)))

    def tile1(t):
        tsl = slice(t * TT, (t + 1) * TT)
        pending = []
        for hd in range(8):
            pending = attn_head(t, hd, pending)
        while pending:
            pending.pop(0)()
        if t + 1 < NT:
            prologue_load(t + 1)
        P.op("sp", lambda e: e.dma_start(out=C.h, in_=io["h1T"][:, tsl].rearrange("(c p) n -> p c n", p=128)),
             reads=[("h1s", t)], writes=[("h", c) for c in range(8)], dma_sem="h")
        tile1_tail(t, tsl)

    def attn_head(t, hd, pending):
        nkeys = (t + 1) * 512
        nblk = (t + 1) * 4
        par = st8["head"] % 2
        st8["head"] += 1
        ac = acc[par]
        bufs = []
        for r in range(2):
            s = st8["kvi"] % NKB
            st8["kvi"] += 1
            P.op("sp", lambda e, s=s, r=r: e.dma_start(out=kbuf[s][:, 0:nkeys], in_=kg[r * 1024 + hd * 128:r * 1024 + (hd + 1) * 128, 0:nkeys]),
                 reads=[("Kd", r, i_) for i_ in range(t + 1)], writes=[("kbuf", s)], dma_sem=("kb", s))
            P.op("sp", lambda e, s=s, r=r: e.dma_start(out=vbuf[s][:, 0:nblk, :], in_=vg[r * 128:(r + 1) * 128, hd, 0:nblk, :]),
                 reads=[("Vd", r, i_) for i_ in range(t + 1)], writes=[("vbuf", s)], dma_sem=("vb", s))
            bufs.append(s)
        blocks = [(r, kb) for r in range(2) for kb in range(nblk)]
        n = len(blocks)
        EB, ZB = 7, 4
        ob = (5, 6)

        def qk(i):
            r, kb = blocks[i]
            s = bufs[r]
            for m in range(2):
                bnk = (i % 2) * 2 + m
                P.op("pe", lambda e, m=m, bnk=bnk: e.matmul(
                    C.ps[bnk][:], lhsT=kbuf[s][m * 64:(m + 1) * 64, kb * 128:(kb + 1) * 128], rhs=q[m * 64:(m + 1) * 64, hd, :],
                    start=True, stop=True), reads=[("kbuf", s), ("q", hd)], writes=[("ps", bnk)], fresh=[("ps", bnk)])
        qk(0)
        qk(1)
        for i in range(n):
            r, kb = blocks[i]
            s = bufs[r]
            pi = st8["pbi"] % NPB
            st8["pbi"] += 1
            pbt = ("pb", pi)
            if pending and i >= 1:
                pending.pop(0)()
            sp_ = i % 2
            P.op("act", lambda e, sp_=sp_, pi=pi: e.activation(out=pb[pi].rearrange("p a b -> p (a b)"), in_=C.ps2[sp_][:], func=AF.Exp, scale=0.125),
                 reads=[("ps", 2 * sp_), ("ps", 2 * sp_ + 1)], writes=[pbt])
            if kb >= t * 4:
                jj = kb - t * 4
                if r == 0:
                    P.op("dve", lambda e, pi=pi, jj=jj: e.tensor_tensor(out=pb[pi], in0=pb[pi],
                                                                        in1=dmask[:, jj, :].unsqueeze(1).to_broadcast([128, 2, 512]), op=ALU.mult),
                         reads=["dmask"], writes=[pbt])
                else:
                    P.op("dve", lambda e, pi=pi: e.tensor_scalar(out=pb[pi], in0=pb[pi], scalar1=vis[:, t % 2:t % 2 + 1], scalar2=None, op0=ALU.mult),
                         reads=["vis"], writes=[pbt])
            if i == 0:
                P.op("dve", lambda e, pi=pi: e.tensor_copy(out=ac, in_=pb[pi][:, 1, :]), reads=[pbt], writes=[("acc", par)])
            else:
                P.op("dve", lambda e, pi=pi: e.tensor_tensor(out=ac, in0=ac, in1=pb[pi][:, 1, :], op=ALU.add), reads=[pbt], writes=[("acc", par)])
            if i + 2 < n:
                qk(i + 2)
            for m in range(2):
                P.op("pe", lambda e, s=s, kb=kb, m=m, pi=pi, i=i: e.matmul(C.ps[ob[m]][:], lhsT=vbuf[s][:, kb, :], rhs=pb[pi][:, m, :],
                                                                         start=(i == 0), stop=(i == n - 1)),
                     reads=[("vbuf", s), pbt], writes=[("ps", ob[m])], fresh=([("ps", ob[m])] if i == 0 else []))
            P.op("pe", lambda e, pi=pi, i=i: e.matmul(C.ps[ZB][:], lhsT=C.ones, rhs=pb[pi][:, 0, :], start=(i == 0), stop=(i == n - 1)),
                 reads=[pbt, "ones"], writes=[("ps", ZB)], fresh=([("ps", ZB)] if i == 0 else []))

        P.op("act", lambda e: e.activation(out=r0, in_=C.ps[ZB][:], func=AF.Ln), reads=[("ps", ZB)], writes=[("r", 0)])
        P.op("dve", lambda e: e.tensor_copy(out=o0, in_=C.ps[ob[0]][:]), reads=[("ps", ob[0])], writes=["o0"])
        P.op("dve", lambda e: e.tensor_copy(out=o1, in_=C.ps[ob[1]][:]), reads=[("ps", ob[1])], writes=["o1"])

        def st_a():
            P.op("act", lambda e: e.activation(out=r0, in_=r0, func=AF.Exp, scale=-1.0), reads=[], writes=[("r", 0)])
            P.op("dve", lambda e: e.tensor_copy(out=accbf, in_=ac), reads=[("acc", par)], writes=["accbf"])
            P.op("pe", lambda e: e.matmul(C.ps[EB][:], lhsT=C.ones, rhs=accbf, start=True, stop=True),
                 reads=["accbf", "ones"], writes=[("ps", EB)], fresh=[("ps", EB)])

        def st_b():
            P.op("act", lambda e: e.activation(out=r1, in_=C.ps[EB][:], func=AF.Ln), reads=[("ps", EB)], writes=[("r", 1)])
            P.op("act", lambda e: e.activation(out=r1, in_=r1, func=AF.Exp, scale=-1.0), reads=[], writes=[("r", 1)])

        def st_c():
            P.op("dve", lambda e: e.tensor_tensor(out=o0, in0=o0, in1=r0, op=ALU.mult), reads=[("r", 0)], writes=["o0"])
            P.op("dve", lambda e: e.tensor_tensor(out=o1, in0=o1, in1=r1, op=ALU.mult), reads=[("r", 1)], writes=["o1"])
            P.op("dve", lambda e: e.scalar_tensor_tensor(out=of, in0=o1, scalar=neglam, in1=o0, op0=ALU.mult, op1=ALU.add),
                 reads=["o0", "o1", "neglam"], writes=["of"])

        def st_d():
            P.op("act", lambda e: e.activation(out=osq, in_=of, func=AF.Square), reads=["of"], writes=["osq"])
            P.op("pe", lambda e: e.matmul(C.ps[EB][:], lhsT=C.ones, rhs=osq, start=True, stop=True), reads=["osq", "ones"],
                 writes=[("ps", EB)], fresh=[("ps", EB)])

        def st_e():
            P.op("dve", lambda e: e.tensor_scalar(out=C.tpre, in0=C.ps[EB][:], scalar1=1.0 / 128, scalar2=EPS, op0=ALU.mult, op1=ALU.add),
                 reads=[("ps", EB)], writes=["tpre"])
            rsqrt(C, C.rstd, C.tpre, "rstd", "tpre")

        def st_f():
            P.op("dve", lambda e: e.scalar_tensor_tensor(out=on[:, hd, :], in0=of, scalar=subg, in1=C.rstd, op0=ALU.mult, op1=ALU.mult),
                 reads=["of", "rstd", "subg"], writes=[("on", hd)])

        while pending:
            pending.pop(0)()
        return [st_a, st_b, st_c, st_d, st_e, st_f]

    def tile1_tail(t, tsl):
        C.bank_i = 0
        for cb in range(2):
            pw, tw = load_piece(C, "wo", io["wo_bf"], 0, 8, cb * 512, 512)
            for j in range(4):
                m = cb * 4 + j
                b = nb(C)
                mm_A(C, b, pw, tw, j * 128, 8, lambda k: on[:, k, :], lambda k: ("on", k))
                P.op("dve", lambda e, m=m, b=b: e.tensor_tensor(out=C.h[:, m, :], in0=C.h[:, m, :], in1=C.ps[b][:], op=ALU.add),
                     reads=[("ps", b)], writes=[("h", m)])
        rmsnorm(C, vecs[:, 8:16])
        ffn_up(C, io["wgu1_bf"], "1")
        if t + 1 < NT:
            prologue_norm(t + 1)
        ffn_down(C, io["wdn1_bf"], "1")
        if t + 1 < NT:
            prologue_q(t + 1)
        rmsnorm(C, vecs[:, 16:24], out_f32=True)
        P.op("pool", lambda e, tsl=tsl: e.dma_start(out=io["outT"][:, tsl].rearrange("(c p) n -> p c n", p=128), in_=C.yout),
             reads=[("hmid", f_) for f_ in range(16)], dma_sem="outo")

    prologue_load(0)
    prologue_norm(0)
    prologue_q(0)
    for t in range(NT):
        tile1(t)
    P.final_waits += [("pool", "outo")]


def build_fused():
    nc = bass.Bass("TRN2", target_bir_lowering=False)
    io = {}
    io["xT"] = dram_in(nc, "xT", [1024, 2 * 4096])
    io["cosT"] = dram_in(nc, "cosT", [128, 2 * 4096])
    io["sinT"] = dram_in(nc, "sinT", [128, 2 * 4096])
    io["vecs0"] = dram_in(nc, "vecs0", [128, 56])
    io["Rm"] = dram_in(nc, "Rm", [128, 128], BF16)
    io["wsT"] = dram_in(nc, "wsT", [128, 1024])
    io["bsb"] = dram_in(nc, "bsb", [128, 1024])
    io["tri"] = dram_in(nc, "tri", [128, 128])
    io["dmask"] = dram_in(nc, "dmask", [128, 4 * 512], BF16)
    io["vis"] = dram_in(nc, "vis", [128, 2])
    io["vecs1"] = dram_in(nc, "vecs1", [128, 25])
    io["lqk"] = dram_in(nc, "lqk", [128, 256])
    wshapes = (("win", [1024, 4096]), ("wout", [2048, 1024]), ("wgu0", [1024, 2 * DFF]), ("wdn0", [DFF, 1024]),
               ("wkvx", [1024, 2048]), ("wqx", [1024, 1024]), ("wo", [1024, 1024]), ("wgu1", [1024, 2 * DFF]),
               ("wdn1", [DFF, 1024]))
    for name, shp in wshapes:
        io[name] = dram_in(nc, name, shp)
        io[name + "_bf"] = nc.dram_tensor(name + "_bf", shp, BF16, kind="Internal").ap()
    io["h1T"] = nc.dram_tensor("h1s", [1024, 4096], F32, kind="Internal").ap()
    io["Kd"] = nc.dram_tensor("Kd", [2048, 4096], BF16, kind="Internal").ap()
    io["Vd"] = nc.dram_tensor("Vd", [256, 32768], BF16, kind="Internal").ap()
    io["outT"] = nc.dram_tensor("outT", [1024, 4096], F32, kind="ExternalOutput").ap()
    P = Prog(nc)
    with contextlib.ExitStack() as es:
        C = _common(nc, es, P, 204 * 1024)
        mark = C.A.off
        P.op("sp", lambda e: e.dma_start(out=C.Rm, in_=io["Rm"]), writes=["Rm"], dma_sem="c0")
        for name, shp in wshapes:
            cast_weight(C, name, io[name], io[name + "_bf"], shp[0], shp[1])
        emit_l0(C, io)
        P.barrier()
        C.A.off = mark
        emit_l1(C, io)
        P.emit_all(es)
    return nc


def _cols(vec, n):
    return np.ascontiguousarray(np.asarray(vec, np.float32).reshape(n, 128).T)


def _rope_perm():
    perm = np.arange(1024)
    for base in range(0, 1024, 64):
        for i in range(8):
            perm[base + i] = base + i + 8
            perm[base + 8 + i] = base + i
    return perm


def _rope_tables(pos):
    half = 8
    inv = np.power(np.float32(500000.0), -np.arange(half, dtype=np.float32) * np.float32(2.0) / np.float32(16))
    ang = pos.astype(np.float32)[None, :] * inv[:, None]
    cos = np.cos(ang).astype(np.float32)
    sin = np.sin(ang).astype(np.float32)
    T = pos.shape[0]
    cT = np.ones((128, T), np.float32)
    sT = np.zeros((128, T), np.float32)
    for m in range(2):
        cT[m * 64:m * 64 + 8] = cos
        cT[m * 64 + 8:m * 64 + 16] = cos
        sT[m * 64:m * 64 + 8] = -sin
        sT[m * 64 + 8:m * 64 + 16] = sin
    return cT, sT


def _own_pos(role):
    return np.concatenate([np.arange(s * 512, (s + 1) * 512) for s in SB_OWN[role]])


def _dmask():
    m = np.zeros((128, 4, 512), np.float32)
    for jj in range(4):
        m[:, jj, :] = (jj * 128 + np.arange(128))[:, None] <= np.arange(512)[None, :]
    return m.reshape(128, -1).astype(ml_dtypes.bfloat16)


def _vis(role):
    v = np.array([1.0 if SB_OWN[1 - role][t] < SB_OWN[role][t] else 0.0 for t in range(2)], np.float32)
    return np.ascontiguousarray(np.broadcast_to(v[None, :], (128, 2)))


_CACHE = {}


def kernel(x, attn_norm_g, ffn_norm_g, gmlp_w_in, gmlp_ln_g, gmlp_ln_b, gmlp_w_s, gmlp_b_s,
           gmlp_w_out, kv_norm_g, w_kv, diff_w_q, diff_lambda_q, diff_lambda_k, diff_sub_g,
           diff_w_o, ffn_w_gu, ffn_w_down, final_norm_g):
    f = lambda a: np.ascontiguousarray(np.asarray(a, dtype=np.float32))
    x = f(x)
    perm = _rope_perm()
    w_kv = f(w_kv)
    wkvx = w_kv
    wq = f(diff_w_q)[0]
    wqx = np.ascontiguousarray(wq)
    Rm = np.zeros((128, 128), np.float32)
    for m_ in range(128):
        if m_ % 64 < 16:
            Rm[perm[m_], m_] = 1.0
    Rm = Rm.astype(ml_dtypes.bfloat16)
    vecs0 = np.concatenate([_cols(attn_norm_g[0], 8), _cols(ffn_norm_g[0], 8), _cols(kv_norm_g, 8),
                            _cols(gmlp_ln_g[0], 16), _cols(gmlp_ln_b[0], 16)], axis=1)
    vecs1 = np.concatenate([_cols(attn_norm_g[1], 8), _cols(ffn_norm_g[1], 8), _cols(final_norm_g, 8),
                            _cols(diff_sub_g[0], 1)], axis=1)
    wsT = np.ascontiguousarray(np.transpose(f(gmlp_w_s)[0], (2, 0, 1)).reshape(128, 1024))
    bsb = np.ascontiguousarray(np.broadcast_to(f(gmlp_b_s)[0].reshape(1, 1024), (128, 1024)))
    tri = np.triu(np.ones((128, 128), np.float32))
    lqk = np.concatenate([f(diff_lambda_q)[0].reshape(-1), f(diff_lambda_k)[0].reshape(-1)])
    lqk = np.ascontiguousarray(np.broadcast_to(lqk[None, :], (128, 256)))
    ropes = [_rope_tables(_own_pos(r)) for r in range(2)]
    dmask = _dmask()

    ncores = 8
    if "nc" not in _CACHE:
        _CACHE["nc"] = build_fused()
    shared = {"vecs0": vecs0, "win": f(gmlp_w_in)[0], "wout": f(gmlp_w_out)[0], "wgu0": f(ffn_w_gu)[0],
              "wdn0": f(ffn_w_down)[0], "wkvx": wkvx, "Rm": Rm, "wsT": wsT, "bsb": bsb, "tri": tri, "vecs1": vecs1,
              "lqk": lqk, "wqx": wqx, "wo": f(diff_w_o)[0], "wgu1": f(ffn_w_gu)[1], "wdn1": f(ffn_w_down)[1]}
    in_maps = []
    for c in range(ncores):
        b, role = c // 2, c % 2
        order = np.concatenate([_own_pos(role), _own_pos(1 - role)])
        d = dict(shared)
        d["xT"] = np.ascontiguousarray(x[b][order].T)
        d["cosT"] = np.ascontiguousarray(np.concatenate([ropes[role][0], ropes[1 - role][0]], axis=1))
        d["sinT"] = np.ascontiguousarray(np.concatenate([ropes[role][1], ropes[1 - role][1]], axis=1))
        d["dmask"] = dmask
        d["vis"] = _vis(role)
        in_maps.append(d)
    res = run_bass_kernel_spmd(_CACHE["nc"], in_maps, core_ids=list(range(ncores))).results
    out = np.empty((4, SEQ, D), np.float32)
    for c in range(ncores):
        b, role = c // 2, c % 2
        out[b][_own_pos(role)] = res[c]["outT"].T
    return out
```

```python
import contextlib
import math
import numpy as np
import ml_dtypes
import concourse.bass as bass
import concourse.mybir as mybir
from concourse.bass_utils import run_bass_kernel_spmd

F32 = mybir.dt.float32
BF16 = mybir.dt.bfloat16
AF = mybir.ActivationFunctionType
ALU = mybir.AluOpType
AX = mybir.AxisListType

D = 1024
SEQ = 8192
NT = 8
TT = 512
DFF = 2816
EPS = 1e-5
SB_OWN = ([0, 3, 4, 7, 8, 11, 12, 15], [1, 2, 5, 6, 9, 10, 13, 14])
LAM_INIT = 0.8 - 0.6 * math.exp(-0.3 * 1)
ENGS = ("pe", "act", "dve", "pool", "sp")


class Op:
    __slots__ = ("eng", "emit", "deps", "signal", "cnt", "dma_sem", "dma_cnt", "idx")


class Prog:
    def __init__(self, nc):
        self.nc = nc
        self.ops = []
        self.by_eng = {e: [] for e in ENGS}
        self.last_w = {}
        self.readers = {}
        self.dma_sem_count = {}
        self.final_waits = []

    def op(self, eng, emit, reads=(), writes=(), dma_sem=None, fresh=()):
        for t in fresh:
            assert self.last_w.get(t) is None or self.readers.get(t), ("overwrite of unread", t)
        o = Op()
        o.eng = eng
        o.emit = emit
        o.signal = False
        o.cnt = 0
        o.dma_sem = dma_sem
        o.dma_cnt = 0
        o.idx = len(self.ops)
        deps = set()
        for r in reads:
            w = self.last_w.get(r)
            if w is not None:
                deps.add(w)
        for t in writes:
            w = self.last_w.get(t)
            if w is not None:
                deps.add(w)
            for rd in self.readers.get(t, ()):
                deps.add(rd)
        o.deps = deps
        if dma_sem is not None:
            c = self.dma_sem_count.get(dma_sem, 0) + 1
            self.dma_sem_count[dma_sem] = c
            o.dma_cnt = c
        for r in reads:
            self.readers.setdefault(r, []).append(o.idx)
        for t in writes:
            self.last_w[t] = o.idx
            self.readers[t] = []
        self.ops.append(o)
        self.by_eng[eng].append(o)
        return o

    def barrier(self):
        deps = set()
        seen_dma = {}
        for o in self.ops:
            if o.dma_sem is not None:
                seen_dma[o.dma_sem] = o.idx
        deps.update(seen_dma.values())
        for e in ENGS:
            for o in reversed(self.by_eng[e]):
                if o.dma_sem is None:
                    deps.add(o.idx)
                    break
        for e in ENGS:
            o = self.op(e, lambda eng: eng.nop())
            o.deps |= deps

    def emit_all(self, es):
        nc = self.nc
        ops = self.ops
        sems = {e: es.enter_context(nc.semaphore("s_" + e)) for e in ENGS}
        dma_sems = {}
        for i, k in enumerate(self.dma_sem_count):
            dma_sems[k] = es.enter_context(nc.semaphore("d%d" % i))
        for o in ops:
            for d in o.deps:
                p = ops[d]
                if p.dma_sem is None:
                    if p.eng == "pe" and o.eng == "pe":
                        continue
                    p.signal = True
        for e in ENGS:
            c = 0
            for o in self.by_eng[e]:
                if o.dma_sem is None and o.signal:
                    c += 1
                    o.cnt = c
        prog = self

        def run(e, eng):
            waited = {}
            for o in prog.by_eng[e]:
                need = {}
                for d in o.deps:
                    p = ops[d]
                    if p.dma_sem is not None:
                        key = ("d", p.dma_sem)
                        val = 16 * p.dma_cnt
                    else:
                        if p.eng == "pe" and e == "pe":
                            continue
                        key = ("e", p.eng)
                        val = p.cnt
                    if val > need.get(key, 0):
                        need[key] = val
                if o.dma_sem is not None and o.dma_cnt > 1:
                    key = ("d", o.dma_sem)
                    need[key] = max(need.get(key, 0), 16 * (o.dma_cnt - 1))
                for key, val in need.items():
                    if waited.get(key, 0) >= val:
                        continue
                    waited[key] = val
                    sem = dma_sems[key[1]] if key[0] == "d" else sems[key[1]]
                    eng.wait_ge(sem, val)
                ins = o.emit(eng)
                if o.dma_sem is not None:
                    ins.then_inc(dma_sems[o.dma_sem], 16)
                elif o.signal:
                    ins.then_inc(sems[e], 1)
            for (e2, key) in prog.final_waits:
                if e2 == e:
                    eng.wait_ge(dma_sems[key], 16 * prog.dma_sem_count[key])

        with nc.Block() as block:
            @block.tensor
            def _(eng):
                run("pe", eng)

            @block.scalar
            def _(eng):
                run("act", eng)

            @block.vector
            def _(eng):
                run("dve", eng)

            @block.gpsimd
            def _(eng):
                run("pool", eng)

            @block.sync
            def _(eng):
                run("sp", eng)


class Arena:
    def __init__(self, nc, es, nbytes):
        self.t = es.enter_context(nc.sbuf_tensor("arena", [128, nbytes // 4], F32))
        self.off = 0
        self.cap = nbytes // 4

    def carve(self, shape, dtype):
        n = int(np.prod(shape))
        nb = n * (4 if dtype == F32 else 2)
        w = (nb + 31) // 32 * 8
        assert self.off + w <= self.cap, ("arena overflow", self.off * 4, w * 4, self.cap * 4)
        ap = self.t[:, self.off:self.off + w]
        self.off += w
        if dtype != F32:
            ap = ap.bitcast(dtype)
        ap = ap[:, 0:n]
        if len(shape) == 2:
            ap = ap.rearrange("p (a b) -> p a b", a=shape[0])
        elif len(shape) == 3:
            ap = ap.rearrange("p (a b c) -> p a b c", a=shape[0], b=shape[1])
        return ap


class Ctx:
    pass


def _common(nc, es, P, arena_bytes):
    C = Ctx()
    C.nc, C.P = nc, P
    C.A = Arena(nc, es, arena_bytes)
    C.ps2 = [es.enter_context(nc.psum_tensor("ps%d" % i, [128, 1024], F32)) for i in range(4)]
    C.ps = [C.ps2[i // 2][:, (i % 2) * 512:(i % 2 + 1) * 512] for i in range(8)]
    C.bank_i = 0
    A = C.A
    C.h = A.carve((8, 512), F32)
    C.xn = A.carve((8, 512), BF16)
    C.sq = A.carve((8, 512), BF16)
    C.tpre = A.carve((512,), F32)
    C.rstd = A.carve((512,), F32)
    C.neghalf = A.carve((512,), F32)
    C.ones = A.carve((128,), BF16)
    C.hmid = A.carve((22, 512), BF16)
    C.yout = C.hmid[:, 0:16, :].rearrange("p a b -> p (a b)").bitcast(F32).rearrange("p (a b) -> p a b", a=8)
    C.sg = [A.carve((512,), F32) for _ in range(2)]
    C.cos = A.carve((512,), F32)
    C.sin = A.carve((512,), F32)
    C.t1 = A.carve((512,), F32)
    C.t2 = A.carve((512,), F32)
    C.khi = [A.carve((512,), BF16) for _ in range(2)]
    C.klo = [A.carve((512,), BF16) for _ in range(2)]
    C.Rm = A.carve((128,), BF16)
    C.hl_i = 0
    C.nslots = 4
    C.slots = [A.carve((4096,), BF16) for _ in range(C.nslots)]
    C.slot_i = 0
    P.op("pool", lambda e: e.memset(C.neghalf, -0.5), writes=["neghalf"])
    P.op("pool", lambda e: e.memset(C.ones, 1.0), writes=["ones"])
    return C


def nb(C):
    b = C.bank_i % 8
    C.bank_i += 1
    return b


def cast_weight(C, name, src, dst, rows, cols, cstep=512, rstep=1024):
    P = C.P
    starts = list(range(0, cols, cstep))
    if name == "win":
        starts = starts[4:] + starts[:4]
    for c0 in starts:
        c1 = min(cols, c0 + cstep)
        for r in range(0, rows, rstep):
            r1 = min(rows, r + rstep)
            P.op("pool", lambda e, r=r, r1=r1, c0=c0, c1=c1: e.dma_start(out=dst[r:r1, c0:c1], in_=src[r:r1, c0:c1]),
                 writes=[("wbf", name, c0 // cstep, r // rstep)], dma_sem=("wc", name, c0 // cstep, r // rstep))


def load_piece(C, name, w, r0, nk, c0, ncols):
    P = C.P
    s = C.slot_i % C.nslots
    C.slot_i += 1
    view = C.slots[s][:, 0:nk * ncols].rearrange("p (k n) -> p k n", k=nk)
    src = w[r0:r0 + nk * 128, c0:c0 + ncols].rearrange("(k p) n -> p k n", p=128)
    rd = [("wbf", name, cb, rb) for cb in range(c0 // 512, (c0 + ncols - 1) // 512 + 1)
          for rb in range(r0 // 1024, (r0 + nk * 128 - 1) // 1024 + 1)]
    P.op("sp", lambda e: e.dma_start(out=view, in_=src), reads=rd,
         writes=[("slot", s)], dma_sem=("slot", s))
    return view, ("slot", s)


def mm_A(C, b, piece, ptok, coff, nk, rhs_fn, rhs_toks, start=True, stop=True):
    P = C.P
    for k in range(nk):
        P.op("pe", lambda e, k=k: e.matmul(C.ps[b][:], lhsT=piece[:, k, coff:coff + 128], rhs=rhs_fn(k),
                                            start=(start and k == 0), stop=(stop and k == nk - 1)),
             reads=[ptok, rhs_toks(k)], writes=[("ps", b)], fresh=([("ps", b)] if (start and k == 0) else []))


def mm_B(C, b, piece, ptok, tc, ncols=512):
    P = C.P
    for k in range(8):
        P.op("pe", lambda e, k=k: e.matmul(C.ps[b][:, 0:ncols], lhsT=C.xn[:, k, tc * 128:(tc + 1) * 128],
                                            rhs=piece[:, k, 0:ncols], start=(k == 0), stop=(k == 7)),
             reads=[ptok, ("xn", k)], writes=[("ps", b)])


def rsqrt(C, out, in_, otok, itok):
    P = C.P
    P.op("act", lambda e: e.activation(out=in_, in_=in_, func=AF.Ln), reads=[], writes=[itok])
    P.op("act", lambda e: e.activation(out=out, in_=in_, func=AF.Exp, scale=-0.5), reads=[itok], writes=[otok])


def rmsnorm(C, gcol, out_f32=False, hbuf=None, htoks=None):
    P = C.P
    hb = C.h if hbuf is None else hbuf
    if htoks is None:
        htoks = lambda c: [("h", c)]
    for c in range(8):
        P.op("act", lambda e, c=c: e.activation(out=C.sq[:, c, :], in_=hb[:, c, :], func=AF.Square),
             reads=htoks(c), writes=[("sq", c)])
    b = nb(C)
    for c in range(8):
        P.op("pe", lambda e, c=c: e.matmul(C.ps[b][:], lhsT=C.ones, rhs=C.sq[:, c, :], start=(c == 0), stop=(c == 7)),
             reads=[("sq", c), "ones"], writes=[("ps", b)])
    P.op("dve", lambda e: e.tensor_scalar(out=C.tpre, in0=C.ps[b][:], scalar1=1.0 / D, scalar2=EPS,
                                          op0=ALU.mult, op1=ALU.add), reads=[("ps", b)], writes=["tpre"])
    rsqrt(C, C.rstd, C.tpre, "rstd", "tpre")
    for c in range(8):
        if out_f32:
            P.op("dve", lambda e, c=c: e.scalar_tensor_tensor(out=C.yout[:, c, :], in0=hb[:, c, :], scalar=gcol[:, c:c + 1],
                                                              in1=C.rstd, op0=ALU.mult, op1=ALU.mult),
                 reads=htoks(c) + ["rstd", "vecs"], writes=[("hmid", 2 * c), ("hmid", 2 * c + 1)])
        else:
            P.op("dve", lambda e, c=c: e.scalar_tensor_tensor(out=C.xn[:, c, :], in0=hb[:, c, :], scalar=gcol[:, c:c + 1],
                                                              in1=C.rstd, op0=ALU.mult, op1=ALU.mult),
                 reads=htoks(c) + ["rstd", "vecs"], writes=[("xn", c)])


def ffn(C, wgu, wdn, lname):
    ffn_up(C, wgu, lname)
    ffn_down(C, wdn, lname)


def ffn_up(C, wgu, lname):
    P = C.P
    f = 0
    gi = 0
    for g0 in range(0, DFF, 512):
        ncols = min(512, DFF - g0)
        pg, tg = load_piece(C, "wgu" + lname, wgu, 0, 8, g0, ncols)
        pu, tu = load_piece(C, "wgu" + lname, wgu, 0, 8, DFF + g0, ncols)
        for j in range(ncols // 128):
            bg = nb(C)
            mm_A(C, bg, pg, tg, j * 128, 8, lambda k: C.xn[:, k, :], lambda k: ("xn", k))
            bu = nb(C)
            mm_A(C, bu, pu, tu, j * 128, 8, lambda k: C.xn[:, k, :], lambda k: ("xn", k))
            s = C.sg[gi % 2]
            stok = ("sg", gi % 2)
            gi += 1
            P.op("act", lambda e, bg=bg, s=s: e.activation(out=s, in_=C.ps[bg][:], func=AF.Silu),
                 reads=[("ps", bg)], writes=[stok])
            P.op("dve", lambda e, bu=bu, s=s, f=f: e.tensor_tensor(out=C.hmid[:, f, :], in0=s, in1=C.ps[bu][:], op=ALU.mult),
                 reads=[("ps", bu), stok], writes=[("hmid", f)])
            f += 1
    assert f == 22


def ffn_down(C, wdn, lname):
    P = C.P
    for cb in range(4):
        banks = [nb(C), nb(C)]
        for kh in range(2):
            pd, td = load_piece(C, "wdn" + lname, wdn, kh * 11 * 128, 11, cb * 256, 256)
            for j in range(2):
                mm_A(C, banks[j], pd, td, j * 128, 11, lambda k, kh=kh: C.hmid[:, kh * 11 + k, :],
                     lambda k, kh=kh: ("hmid", kh * 11 + k), start=(kh == 0), stop=(kh == 1))
        for j in range(2):
            m = cb * 2 + j
            P.op("dve", lambda e, m=m, b=banks[j]: e.tensor_tensor(out=C.h[:, m, :], in0=C.h[:, m, :], in1=C.ps[b][:], op=ALU.add),
                 reads=[("ps", banks[j])], writes=[("h", m)])


def rope_heads(C, items, out_fn, add_eng="pool"):
    P = C.P
    prev = None

    def finish(b1, hd, hl):
        b2 = nb(C)
        P.op("pe", lambda e: e.matmul(C.ps[b2][:], lhsT=C.Rm, rhs=C.khi[hl], start=True, stop=False),
             reads=[("khi", hl), "Rm"], writes=[("ps", b2)], fresh=[("ps", b2)])
        P.op("pe", lambda e: e.matmul(C.ps[b2][:], lhsT=C.Rm, rhs=C.klo[hl], start=False, stop=True),
             reads=[("klo", hl), "Rm"], writes=[("ps", b2)])
        ap, tok = out_fn(hd)
        rope_evac(C, b1, b2, ap, tok, add_eng)

    for piece, ptok, coff, hd in items:
        b1 = nb(C)
        mm_A(C, b1, piece, ptok, coff, 8, lambda k: C.xn[:, k, :], lambda k: ("xn", k))
        hl = C.hl_i % 2
        C.hl_i += 1
        P.op("act", lambda e, b1=b1, hl=hl: e.copy(out=C.khi[hl], in_=C.ps[b1][:]), reads=[("ps", b1)], writes=[("khi", hl)])
        P.op("dve", lambda e, b1=b1, hl=hl: e.tensor_tensor(out=C.klo[hl], in0=C.ps[b1][:], in1=C.khi[hl], op=ALU.subtract),
             reads=[("ps", b1), ("khi", hl)], writes=[("klo", hl)])
        if prev is not None:
            finish(*prev)
        prev = (b1, hd, hl)
    finish(*prev)


def rope_evac(C, b1, b2, out_ap, out_tok, add_eng="pool"):
    P = C.P
    P.op("dve", lambda e: e.tensor_tensor(out=C.t1, in0=C.ps[b1][:], in1=C.cos, op=ALU.mult),
         reads=[("ps", b1), "cos"], writes=["t1"])
    P.op("dve", lambda e: e.tensor_tensor(out=C.t2, in0=C.ps[b2][:], in1=C.sin, op=ALU.mult),
         reads=[("ps", b2), "sin"], writes=["t2"])
    P.op(add_eng, lambda e: e.tensor_tensor(out=out_ap, in0=C.t1, in1=C.t2, op=ALU.add),
         reads=["t1", "t2"], writes=[out_tok])


def load_rope(C, cosT, sinT, t):
    P = C.P
    P.op("sp", lambda e: e.dma_start(out=C.cos, in_=cosT[:, t * TT:(t + 1) * TT]), writes=["cos"], dma_sem="cos")
    P.op("sp", lambda e: e.dma_start(out=C.sin, in_=sinT[:, t * TT:(t + 1) * TT]), writes=["sin"], dma_sem="sin")


def dram_in(nc, name, shape, dt=F32):
    return nc.dram_tensor(name, list(shape), dt, kind="ExternalInput").ap()


def emit_l0(C, io):
    P, nc, A = C.P, C.nc, C.A
    u = A.carve((16, 512), BF16)
    v = A.carve((4, 2048), F32)
    vh = A.carve((4, 2048), BF16)
    st = A.carve((4, 24), F32)
    mv = A.carve((4, 2), F32)
    rs = A.carve((4, 1), F32)
    tmpg = [A.carve((512,), F32) for _ in range(2)]
    ksb = A.carve((8, 512), BF16)
    vsb = A.carve((8, 4, 128), BF16)
    vecs = A.carve((56,), F32)
    wsb = A.carve((8, 128), BF16)
    wsf = v[:, 0, 0:1024].rearrange("p (g t) -> p g t", g=8)
    bsb = v[:, 0, 1024:2048].rearrange("p (g t) -> p g t", g=8)
    tri = A.carve((128,), F32)
    Bc = A.carve((16, 128), F32)

    P.op("sp", lambda e: e.dma_start(out=vecs, in_=io["vecs0"]), writes=["vecs"], dma_sem="c0")
    P.op("sp", lambda e: e.dma_start(out=wsf, in_=io["wsT"].rearrange("p (g t) -> p g t", g=8)), writes=["wsf"], dma_sem="c0")
    P.op("sp", lambda e: e.dma_start(out=bsb, in_=io["bsb"].rearrange("p (g t) -> p g t", g=8)), writes=["bsb"], dma_sem="c0")
    P.op("sp", lambda e: e.dma_start(out=tri, in_=io["tri"]), writes=["tri"], dma_sem="c0")
    P.op("dve", lambda e: e.tensor_tensor(out=wsb, in0=wsf, in1=tri.unsqueeze(1).to_broadcast([128, 8, 128]), op=ALU.mult),
         reads=["wsf", "tri", ("v", 0, 0), ("v", 0, 1)], writes=["wsb"])
    for half in range(2):
        b = nb(C)
        P.op("pe", lambda e, b=b, half=half: e.matmul(C.ps[b][:], lhsT=C.ones, rhs=wsb[:, half * 4:(half + 1) * 4, :],
                                                      start=True, stop=True), reads=["wsb", "ones"], writes=[("ps", b)])
        for gg in range(4):
            g = half * 4 + gg
            for cc in range(2):
                c = g * 2 + cc
                P.op("dve", lambda e, b=b, gg=gg, g=g, c=c: e.scalar_tensor_tensor(
                    out=Bc[:, c, :], in0=C.ps[b][:, gg * 128:(gg + 1) * 128], scalar=vecs[:, 40 + c:41 + c],
                    in1=bsb[:, g, :], op0=ALU.mult, op1=ALU.add), reads=[("ps", b), "vecs", "bsb", ("v", 0, 2), ("v", 0, 3)], writes=[("Bc", c)])

    def tile0(t):
        o_, i_ = t // NT, t % NT
        tsl = slice(t * TT, (t + 1) * TT)
        isl = slice(i_ * TT, (i_ + 1) * TT)
        P.op("sp", lambda e, tsl=tsl: e.dma_start(out=C.h, in_=io["xT"][:, tsl].rearrange("(c p) n -> p c n", p=128)),
             writes=[("h", c) for c in range(8)], dma_sem="h")
        load_rope(C, io["cosT"], io["sinT"], t)
        rmsnorm(C, vecs[:, 0:8])
        for nbk in range(4):
            pw, tw = load_piece(C, "win", io["win_bf"], 0, 8, 2048 + nbk * 512, 512)
            for tc in range(4):
                b = nb(C)
                mm_B(C, b, pw, tw, tc)
                P.op("act", lambda e, b=b, tc=tc, nbk=nbk: e.activation(out=v[:, tc, nbk * 512:(nbk + 1) * 512], in_=C.ps[b][:], func=AF.Gelu),
                     reads=[("ps", b)], writes=[("v", tc, nbk)])
        for tc in range(4):
            for nbk in range(4):
                P.op("dve", lambda e, tc=tc, nbk=nbk: e.bn_stats(out=st[:, tc, nbk * 6:(nbk + 1) * 6], in_=v[:, tc, nbk * 512:(nbk + 1) * 512]),
                     reads=[("v", tc, nbk)], writes=[("st", tc, nbk)])
            P.op("dve", lambda e, tc=tc: e.bn_aggr(out=mv[:, tc, :], in_=st[:, tc, :]),
                 reads=[("st", tc, n_) for n_ in range(4)], writes=[("mv", tc)])
            P.op("dve", lambda e, tc=tc: e.tensor_scalar(out=rs[:, tc, :], in0=mv[:, tc, 1:2], scalar1=EPS, scalar2=None, op0=ALU.add),
                 reads=[("mv", tc)], writes=[("rs0", tc)])
            if t == 0:
                P.op("act", lambda e, tc=tc: e.activation(out=rs[:, tc, :], in_=rs[:, tc, :], func=AF.Ln), reads=[], writes=[("rs0", tc)])
                P.op("act", lambda e, tc=tc: e.activation(out=rs[:, tc, :], in_=rs[:, tc, :], func=AF.Exp, scale=-0.5), reads=[("rs0", tc)], writes=[("rs", tc)])
            else:
                P.op("pool", lambda e, tc=tc: e.tensor_tensor(out=rs[:, tc, :], in0=rs[:, tc, :], in1=C.neghalf[:, 0:1], op=ALU.pow),
                     reads=[("rs0", tc), "neghalf"], writes=[("rs", tc)])
            P.op("dve", lambda e, tc=tc: e.tensor_scalar(out=vh[:, tc, :], in0=v[:, tc, :], scalar1=mv[:, tc, 0:1], scalar2=rs[:, tc, :],
                                                         op0=ALU.subtract, op1=ALU.mult),
                 reads=[("v", tc, n_) for n_ in range(4)] + [("mv", tc), ("rs", tc)], writes=[("vh", tc)])
        for pc in range(4):
            pw, tw = load_piece(C, "win", io["win_bf"], 0, 8, pc * 512, 512)
            for j in range(4):
                c = pc * 4 + j
                b = nb(C)
                mm_A(C, b, pw, tw, j * 128, 8, lambda k: C.xn[:, k, :], lambda k: ("xn", k))
                P.op("act", lambda e, b=b, c=c: e.activation(out=u[:, c, :], in_=C.ps[b][:], func=AF.Gelu),
                     reads=[("ps", b)], writes=[("u", c)])
        for c in range(16):
            b = nb(C)
            for tc in range(4):
                P.op("pe", lambda e, b=b, c=c, tc=tc: e.matmul(C.ps[b][:, tc * 128:(tc + 1) * 128], lhsT=vh[:, tc, c * 128:(c + 1) * 128],
                                                               rhs=wsb[:, c // 2, :], start=True, stop=True),
                     reads=[("vh", tc), "wsb"], writes=[("ps", b)])
            tg = tmpg[c % 2]
            P.op("dve", lambda e, b=b, c=c, tg=tg: e.scalar_tensor_tensor(
                out=tg.rearrange("p (a t) -> p a t", a=4), in0=C.ps[b][:].rearrange("p (a t) -> p a t", a=4),
                scalar=vecs[:, 24 + c:25 + c], in1=Bc[:, c, :].unsqueeze(1).to_broadcast([128, 4, 128]),
                op0=ALU.mult, op1=ALU.add), reads=[("ps", b), ("Bc", c), "vecs"], writes=[("tmpg", c % 2)])
            P.op("dve" if t == 0 else "pool", lambda e, c=c, tg=tg: e.tensor_tensor(out=u[:, c, :], in0=tg, in1=u[:, c, :], op=ALU.mult),
                 reads=[("tmpg", c % 2)], writes=[("u", c)])
        for cb in range(2):
            banks = [nb(C) for _ in range(4)]
            for kh in range(2):
                pw, tw = load_piece(C, "wout", io["wout_bf"], kh * 1024, 8, cb * 512, 512)
                for j in range(4):
                    mm_A(C, banks[j], pw, tw, j * 128, 8, lambda k, kh=kh: u[:, kh * 8 + k, :], lambda k, kh=kh: ("u", kh * 8 + k),
                         start=(kh == 0), stop=(kh == 1))
            for j in range(4):
                m = cb * 4 + j
                P.op("dve", lambda e, m=m, b=banks[j]: e.tensor_tensor(out=C.h[:, m, :], in0=C.h[:, m, :], in1=C.ps[b][:], op=ALU.add),
                     reads=[("ps", banks[j])], writes=[("h", m)])
        rmsnorm(C, vecs[:, 8:16])
        ffn(C, io["wgu0_bf"], io["wdn0_bf"], "0")
        if o_ == 0:
            P.op("pool", lambda e, isl=isl: e.dma_start(out=io["h1T"][:, isl].rearrange("(c p) n -> p c n", p=128), in_=C.h),
                 reads=[("h", c) for c in range(8)], writes=[("h1s", i_)], dma_sem="h1o")
        rmsnorm(C, vecs[:, 16:24])
        items = []
        for hg in range(2):
            pk, tk = load_piece(C, "wkvx", io["wkvx_bf"], 0, 8, hg * 512, 512)
            items += [(pk, tk, j * 128, hg * 4 + j) for j in range(4)]
        rope_heads(C, items, lambda hd: (ksb[:, hd, :], ("ksb", hd)), add_eng=("dve" if t == 0 else "pool"))
        P.op("pool", lambda e, isl=isl, o_=o_: e.dma_start(out=io["Kd"][o_ * 1024:(o_ + 1) * 1024, isl].rearrange("(c p) n -> p c n", p=128), in_=ksb),
             reads=[("ksb", hd) for hd in range(8)], writes=[("Kd", o_, i_)], dma_sem="ko")
        for vb in range(2):
            pv, tv = load_piece(C, "wkvx", io["wkvx_bf"], 0, 8, 1024 + vb * 512, 512)
            for tc in range(4):
                b = nb(C)
                mm_B(C, b, pv, tv, tc)
                P.op("act", lambda e, b=b, vb=vb, tc=tc: e.copy(out=vsb[:, vb * 4:(vb + 1) * 4, tc, :],
                                                                in_=C.ps[b][:].rearrange("p (h e) -> p h e", h=4)),
                     reads=[("ps", b)], writes=[("vsb", vb, tc)])
        vd4 = io["Vd"][o_ * 128:(o_ + 1) * 128, :].rearrange("p (h b e) -> p h b e", h=8, b=32)
        P.op("pool", lambda e, i_=i_, vd4=vd4: e.dma_start(out=vd4[:, :, i_ * 4:(i_ + 1) * 4, :], in_=vsb),
             reads=[("vsb", vb, tc) for vb in range(2) for tc in range(4)], writes=[("Vd", o_, i_)], dma_sem="vo")

    for t in range(2 * NT):
        tile0(t)


def emit_l1(C, io):
    P, nc, A = C.P, C.nc, C.A
    q = A.carve((8, 512), BF16)
    on = A.carve((8, 512), BF16)
    NKB = 3
    k0_off = A.off
    kbuf = [A.carve((4096,), BF16) for _ in range(NKB)]
    vbuf = [A.carve((32, 128), BF16) for _ in range(NKB)]
    NPB = 3
    pb = [A.carve((2, 512), BF16) for _ in range(NPB)]
    acc = [A.carve((512,), F32) for _ in range(2)]
    accbf = A.carve((512,), BF16)
    dmask = A.carve((4, 512), BF16)
    vis = A.carve((2,), F32)
    r0 = A.carve((512,), F32)
    r1 = A.carve((512,), F32)
    o0 = A.carve((512,), F32)
    o1 = A.carve((512,), F32)
    of = A.carve((512,), F32)
    osq = A.carve((512,), BF16)
    vecs = A.carve((32,), F32)
    lqk = A.carve((256,), F32)
    lsm = A.carve((8,), F32)

    P.op("sp", lambda e: e.dma_start(out=vecs[:, 0:25], in_=io["vecs1"]), writes=["vecs"], dma_sem="c0")
    P.op("sp", lambda e: e.dma_start(out=lqk, in_=io["lqk"]), writes=["lqk"], dma_sem="c0")
    P.op("sp", lambda e: e.dma_start(out=dmask, in_=io["dmask"].rearrange("p (b q) -> p b q", b=4)), writes=["dmask"], dma_sem="c0")
    P.op("sp", lambda e: e.dma_start(out=vis, in_=io["vis"]), writes=["vis"], dma_sem="c0")
    P.op("dve", lambda e: e.tensor_tensor(out=lqk[:, 0:128], in0=lqk[:, 0:128], in1=lqk[:, 128:256], op=ALU.mult),
         reads=["lqk"], writes=["lqk2"])
    P.op("dve", lambda e: e.tensor_reduce(out=lsm[:, 0:2], in_=lqk[:, 0:128].rearrange("p (a d) -> p a d", a=2), axis=AX.X, op=ALU.add),
         reads=["lqk2"], writes=["ls0"])
    P.op("act", lambda e: e.activation(out=lsm[:, 2:4], in_=lsm[:, 0:2], func=AF.Exp), reads=["ls0"], writes=["ls1"])
    P.op("dve", lambda e: e.tensor_tensor(out=lsm[:, 4:5], in0=lsm[:, 3:4], in1=lsm[:, 2:3], op=ALU.subtract), reads=["ls1"], writes=["ls2"])
    P.op("dve", lambda e: e.tensor_scalar(out=lsm[:, 5:6], in0=lsm[:, 4:5], scalar1=-LAM_INIT, scalar2=None, op0=ALU.add), reads=["ls2"], writes=["neglam"])
    P.op("dve", lambda e: e.tensor_scalar(out=lsm[:, 6:7], in0=vecs[:, 24:25], scalar1=1.0 - LAM_INIT, scalar2=None, op0=ALU.mult), reads=["vecs"], writes=["subg"])
    neglam = lsm[:, 5:6]
    subg = lsm[:, 6:7]

    kg = io["Kd"]
    vg = io["Vd"].rearrange("r (h b e) -> r h b e", h=8, b=32)
    st8 = {"kvi": 0, "pbi": 0, "head": 0}

    hq = A.t[:, k0_off:k0_off + 4096]
    hq = hq.rearrange("p (a b) -> p a b", a=8)
    hq_toks = lambda c: [("kbuf", 0), ("kbuf", 1)]

    def prologue_load(t):
        tsl = slice(t * TT, (t + 1) * TT)
        P.op("sp", lambda e: e.dma_start(out=hq, in_=io["h1T"][:, tsl].rearrange("(c p) n -> p c n", p=128)),
             reads=[("h1s", t)], writes=[("kbuf", 0), ("kbuf", 1)], dma_sem="hq")
        load_rope(C, io["cosT"], io["sinT"], t)

    def prologue_norm(t):
        rmsnorm(C, vecs[:, 0:8], hbuf=hq, htoks=hq_toks)

    def prologue_q(t):
        items = []
        for hg in range(2):
            pq, tq = load_piece(C, "wqx", io["wqx_bf"], 0, 8, hg * 512, 512)
            items += [(pq, tq, j * 128, hg * 4 + j) for j in range(4)]
        rope_heads(C, items, lambda hd: (q[:, hd, :], ("q", hd)))

    def tile1(t):
        tsl = slice(t * TT, (t + 1) * TT)
        pending = []
        for hd in range(8):
            pending = attn_head(t, hd, pending)
        while pending:
            pending.pop(0)()
        if t + 1 < NT:
            prologue_load(t + 1)
        P.op("sp", lambda e: e.dma_start(out=C.h, in_=io["h1T"][:, tsl].rearrange("(c p) n -> p c n", p=128)),
             reads=[("h1s", t)], writes=[("h", c) for c in range(8)], dma_sem="h")
        tile1_tail(t, tsl)

    def attn_head(t, hd, pending):
        nkeys = (t + 1) * 512
        nblk = (t + 1) * 4
        par = st8["head"] % 2
        st8["head"] += 1
        ac = acc[par]
        bufs = []
        for r in range(2):
            s = st8["kvi"] % NKB
            st8["kvi"] += 1
            P.op("sp", lambda e, s=s, r=r: e.dma_start(out=kbuf[s][:, 0:nkeys], in_=kg[r * 1024 + hd * 128:r * 1024 + (hd + 1) * 128, 0:nkeys]),
                 reads=[("Kd", r, i_) for i_ in range(t + 1)], writes=[("kbuf", s)], dma_sem=("kb", s))
            P.op("sp", lambda e, s=s, r=r: e.dma_start(out=vbuf[s][:, 0:nblk, :], in_=vg[r * 128:(r + 1) * 128, hd, 0:nblk, :]),
                 reads=[("Vd", r, i_) for i_ in range(t + 1)], writes=[("vbuf", s)], dma_sem=("vb", s))
            bufs.append(s)
        blocks = [(r, kb) for r in range(2) for kb in range(nblk)]
        n = len(blocks)
        EB, ZB = 7, 4
        ob = (5, 6)

        def col0(i):
            r, kb = blocks[i]
            return 128 * (kb - t * 4) if (r == 0 and kb >= t * 4) else 0

        def qk(i):
            r, kb = blocks[i]
            s = bufs[r]
            c0 = col0(i)
            for m in range(2):
                bnk = (i % 2) * 2 + m
                P.op("pe", lambda e, m=m, bnk=bnk: e.matmul(
                    C.ps[bnk][:, c0:], lhsT=kbuf[s][m * 64:(m + 1) * 64, kb * 128:(kb + 1) * 128], rhs=q[m * 64:(m + 1) * 64, hd, c0:],
                    start=True, stop=True), reads=[("kbuf", s), ("q", hd)], writes=[("ps", bnk)], fresh=[("ps", bnk)])
        qk(0)
        qk(1)
        for i in range(n):
            r, kb = blocks[i]
            s = bufs[r]
            pi = st8["pbi"] % NPB
            st8["pbi"] += 1
            pbt = ("pb", pi)
            if pending and i >= 1:
                pending.pop(0)()
            sp_ = i % 2
            c0 = col0(i)
            if c0 == 0:
                P.op("act", lambda e, sp_=sp_, pi=pi: e.activation(out=pb[pi].rearrange("p a b -> p (a b)"), in_=C.ps2[sp_][:], func=AF.Exp, scale=0.125),
                     reads=[("ps", 2 * sp_), ("ps", 2 * sp_ + 1)], writes=[pbt])
            else:
                P.op("act", lambda e, sp_=sp_, pi=pi, c0=c0: e.activation(out=pb[pi][:, :, c0:],
                                                                          in_=C.ps2[sp_][:].rearrange("p (a b) -> p a b", a=2)[:, :, c0:],
                                                                          func=AF.Exp, scale=0.125),
                     reads=[("ps", 2 * sp_), ("ps", 2 * sp_ + 1)], writes=[pbt])
            if kb >= t * 4:
                jj = kb - t * 4
                if r == 0:
                    P.op("dve", lambda e, pi=pi, jj=jj, c0=c0: e.tensor_tensor(out=pb[pi][:, :, c0:], in0=pb[pi][:, :, c0:],
                                                                               in1=dmask[:, jj, c0:].unsqueeze(1).to_broadcast([128, 2, 512 - c0]), op=ALU.mult),
                         reads=["dmask"], writes=[pbt])
                else:
                    P.op("dve", lambda e, pi=pi: e.tensor_scalar(out=pb[pi], in0=pb[pi], scalar1=vis[:, t % 2:t % 2 + 1], scalar2=None, op0=ALU.mult),
                         reads=["vis"], writes=[pbt])
            if i == 0:
                P.op("dve", lambda e, pi=pi: e.tensor_copy(out=ac, in_=pb[pi][:, 1, :]), reads=[pbt], writes=[("acc", par)])
            else:
                P.op("dve", lambda e, pi=pi, c0=c0: e.tensor_tensor(out=ac[:, c0:], in0=ac[:, c0:], in1=pb[pi][:, 1, c0:], op=ALU.add), reads=[pbt], writes=[("acc", par)])
            if i + 2 < n:
                qk(i + 2)
            for m in range(2):
                P.op("pe", lambda e, s=s, kb=kb, m=m, pi=pi, i=i, c0=c0: e.matmul(C.ps[ob[m]][:, c0:], lhsT=vbuf[s][:, kb, :], rhs=pb[pi][:, m, c0:],
                                                                                start=(i == 0), stop=(i == n - 1)),
                     reads=[("vbuf", s), pbt], writes=[("ps", ob[m])], fresh=([("ps", ob[m])] if i == 0 else []))
            P.op("pe", lambda e, pi=pi, i=i, c0=c0: e.matmul(C.ps[ZB][:, c0:], lhsT=C.ones, rhs=pb[pi][:, 0, c0:], start=(i == 0), stop=(i == n - 1)),
                 reads=[pbt, "ones"], writes=[("ps", ZB)], fresh=([("ps", ZB)] if i == 0 else []))

        P.op("act", lambda e: e.activation(out=r0, in_=C.ps[ZB][:], func=AF.Ln), reads=[("ps", ZB)], writes=[("r", 0)])
        P.op("dve", lambda e: e.tensor_copy(out=o0, in_=C.ps[ob[0]][:]), reads=[("ps", ob[0])], writes=["o0"])
        P.op("dve", lambda e: e.tensor_copy(out=o1, in_=C.ps[ob[1]][:]), reads=[("ps", ob[1])], writes=["o1"])

        def st_a():
            P.op("act", lambda e: e.activation(out=r0, in_=r0, func=AF.Exp, scale=-1.0), reads=[], writes=[("r", 0)])
            P.op("dve", lambda e: e.tensor_copy(out=accbf, in_=ac), reads=[("acc", par)], writes=["accbf"])
            P.op("pe", lambda e: e.matmul(C.ps[EB][:], lhsT=C.ones, rhs=accbf, start=True, stop=True),
                 reads=["accbf", "ones"], writes=[("ps", EB)], fresh=[("ps", EB)])

        def st_b():
            P.op("act", lambda e: e.activation(out=r1, in_=C.ps[EB][:], func=AF.Ln), reads=[("ps", EB)], writes=[("r", 1)])
            P.op("act", lambda e: e.activation(out=r1, in_=r1, func=AF.Exp, scale=-1.0), reads=[], writes=[("r", 1)])

        def st_c():
            P.op("dve", lambda e: e.tensor_tensor(out=o0, in0=o0, in1=r0, op=ALU.mult), reads=[("r", 0)], writes=["o0"])
            P.op("dve", lambda e: e.tensor_tensor(out=o1, in0=o1, in1=r1, op=ALU.mult), reads=[("r", 1)], writes=["o1"])
            P.op("dve", lambda e: e.scalar_tensor_tensor(out=of, in0=o1, scalar=neglam, in1=o0, op0=ALU.mult, op1=ALU.add),
                 reads=["o0", "o1", "neglam"], writes=["of"])

        def st_d():
            P.op("act", lambda e: e.activation(out=osq, in_=of, func=AF.Square), reads=["of"], writes=["osq"])
            P.op("pe", lambda e: e.matmul(C.ps[EB][:], lhsT=C.ones, rhs=osq, start=True, stop=True), reads=["osq", "ones"],
                 writes=[("ps", EB)], fresh=[("ps", EB)])

        def st_e():
            P.op("dve", lambda e: e.tensor_scalar(out=C.tpre, in0=C.ps[EB][:], scalar1=1.0 / 128, scalar2=EPS, op0=ALU.mult, op1=ALU.add),
                 reads=[("ps", EB)], writes=["tpre"])
            rsqrt(C, C.rstd, C.tpre, "rstd", "tpre")

        def st_f():
            P.op("dve", lambda e: e.scalar_tensor_tensor(out=on[:, hd, :], in0=of, scalar=subg, in1=C.rstd, op0=ALU.mult, op1=ALU.mult),
                 reads=["of", "rstd", "subg"], writes=[("on", hd)])

        while pending:
            pending.pop(0)()
        return [st_a, st_b, st_c, st_d, st_e, st_f]

    def tile1_tail(t, tsl):
        C.bank_i = 0
        for cb in range(2):
            pw, tw = load_piece(C, "wo", io["wo_bf"], 0, 8, cb * 512, 512)
            for j in range(4):
                m = cb * 4 + j
                b = nb(C)
                mm_A(C, b, pw, tw, j * 128, 8, lambda k: on[:, k, :], lambda k: ("on", k))
                P.op("dve", lambda e, m=m, b=b: e.tensor_tensor(out=C.h[:, m, :], in0=C.h[:, m, :], in1=C.ps[b][:], op=ALU.add),
                     reads=[("ps", b)], writes=[("h", m)])
        rmsnorm(C, vecs[:, 8:16])
        ffn_up(C, io["wgu1_bf"], "1")
        if t + 1 < NT:
            prologue_norm(t + 1)
        ffn_down(C, io["wdn1_bf"], "1")
        if t + 1 < NT:
            prologue_q(t + 1)
        rmsnorm(C, vecs[:, 16:24], out_f32=True)
        P.op("pool", lambda e, tsl=tsl: e.dma_start(out=io["outT"][:, tsl].rearrange("(c p) n -> p c n", p=128), in_=C.yout),
             reads=[("hmid", f_) for f_ in range(16)], dma_sem="outo")

    prologue_load(0)
    prologue_norm(0)
    prologue_q(0)
    for t in range(NT):
        tile1(t)
    P.final_waits += [("pool", "outo")]


def build_fused():
    nc = bass.Bass("TRN2", target_bir_lowering=False)
    io = {}
    io["xT"] = dram_in(nc, "xT", [1024, 2 * 4096])
    io["cosT"] = dram_in(nc, "cosT", [128, 2 * 4096])
    io["sinT"] = dram_in(nc, "sinT", [128, 2 * 4096])
    io["vecs0"] = dram_in(nc, "vecs0", [128, 56])
    io["Rm"] = dram_in(nc, "Rm", [128, 128], BF16)
    io["wsT"] = dram_in(nc, "wsT", [128, 1024])
    io["bsb"] = dram_in(nc, "bsb", [128, 1024])
    io["tri"] = dram_in(nc, "tri", [128, 128])
    io["dmask"] = dram_in(nc, "dmask", [128, 4 * 512], BF16)
    io["vis"] = dram_in(nc, "vis", [128, 2])
    io["vecs1"] = dram_in(nc, "vecs1", [128, 25])
    io["lqk"] = dram_in(nc, "lqk", [128, 256])
    wshapes = (("win", [1024, 4096]), ("wout", [2048, 1024]), ("wgu0", [1024, 2 * DFF]), ("wdn0", [DFF, 1024]),
               ("wkvx", [1024, 2048]), ("wqx", [1024, 1024]), ("wo", [1024, 1024]), ("wgu1", [1024, 2 * DFF]),
               ("wdn1", [DFF, 1024]))
    for name, shp in wshapes:
        io[name] = dram_in(nc, name, shp)
        io[name + "_bf"] = nc.dram_tensor(name + "_bf", shp, BF16, kind="Internal").ap()
    io["h1T"] = nc.dram_tensor("h1s", [1024, 4096], F32, kind="Internal").ap()
    io["Kd"] = nc.dram_tensor("Kd", [2048, 4096], BF16, kind="Internal").ap()
    io["Vd"] = nc.dram_tensor("Vd", [256, 32768], BF16, kind="Internal").ap()
    io["outT"] = nc.dram_tensor("outT", [1024, 4096], F32, kind="ExternalOutput").ap()
    P = Prog(nc)
    with contextlib.ExitStack() as es:
        C = _common(nc, es, P, 204 * 1024)
        mark = C.A.off
        P.op("sp", lambda e: e.dma_start(out=C.Rm, in_=io["Rm"]), writes=["Rm"], dma_sem="c0")
        for name, shp in wshapes:
            cast_weight(C, name, io[name], io[name + "_bf"], shp[0], shp[1])
        emit_l0(C, io)
        P.barrier()
        C.A.off = mark
        emit_l1(C, io)
        P.emit_all(es)
    return nc


def _cols(vec, n):
    return np.ascontiguousarray(np.asarray(vec, np.float32).reshape(n, 128).T)


def _rope_perm():
    perm = np.arange(1024)
    for base in range(0, 1024, 64):
        for i in range(8):
            perm[base + i] = base + i + 8
            perm[base + 8 + i] = base + i
    return perm


def _rope_tables(pos):
    half = 8
    inv = np.power(np.float32(500000.0), -np.arange(half, dtype=np.float32) * np.float32(2.0) / np.float32(16))
    ang = pos.astype(np.float32)[None, :] * inv[:, None]
    cos = np.cos(ang).astype(np.float32)
    sin = np.sin(ang).astype(np.float32)
    T = pos.shape[0]
    cT = np.ones((128, T), np.float32)
    sT = np.zeros((128, T), np.float32)
    for m in range(2):
        cT[m * 64:m * 64 + 8] = cos
        cT[m * 64 + 8:m * 64 + 16] = cos
        sT[m * 64:m * 64 + 8] = -sin
        sT[m * 64 + 8:m * 64 + 16] = sin
    return cT, sT


def _own_pos(role):
    return np.concatenate([np.arange(s * 512, (s + 1) * 512) for s in SB_OWN[role]])


def _dmask():
    m = np.zeros((128, 4, 512), np.float32)
    for jj in range(4):
        m[:, jj, :] = (jj * 128 + np.arange(128))[:, None] <= np.arange(512)[None, :]
    return m.reshape(128, -1).astype(ml_dtypes.bfloat16)


def _vis(role):
    v = np.array([1.0 if SB_OWN[1 - role][t] < SB_OWN[role][t] else 0.0 for t in range(2)], np.float32)
    return np.ascontiguousarray(np.broadcast_to(v[None, :], (128, 2)))


_CACHE = {}


def kernel(x, attn_norm_g, ffn_norm_g, gmlp_w_in, gmlp_ln_g, gmlp_ln_b, gmlp_w_s, gmlp_b_s,
           gmlp_w_out, kv_norm_g, w_kv, diff_w_q, diff_lambda_q, diff_lambda_k, diff_sub_g,
           diff_w_o, ffn_w_gu, ffn_w_down, final_norm_g):
    f = lambda a: np.ascontiguousarray(np.asarray(a, dtype=np.float32))
    x = f(x)
    perm = _rope_perm()
    w_kv = f(w_kv)
    wkvx = w_kv
    wq = f(diff_w_q)[0]
    wqx = np.ascontiguousarray(wq)
    Rm = np.zeros((128, 128), np.float32)
    for m_ in range(128):
        if m_ % 64 < 16:
            Rm[perm[m_], m_] = 1.0
    Rm = Rm.astype(ml_dtypes.bfloat16)
    vecs0 = np.concatenate([_cols(attn_norm_g[0], 8), _cols(ffn_norm_g[0], 8), _cols(kv_norm_g, 8),
                            _cols(gmlp_ln_g[0], 16), _cols(gmlp_ln_b[0], 16)], axis=1)
    vecs1 = np.concatenate([_cols(attn_norm_g[1], 8), _cols(ffn_norm_g[1], 8), _cols(final_norm_g, 8),
                            _cols(diff_sub_g[0], 1)], axis=1)
    wsT = np.ascontiguousarray(np.transpose(f(gmlp_w_s)[0], (2, 0, 1)).reshape(128, 1024))
    bsb = np.ascontiguousarray(np.broadcast_to(f(gmlp_b_s)[0].reshape(1, 1024), (128, 1024)))
    tri = np.triu(np.ones((128, 128), np.float32))
    lqk = np.concatenate([f(diff_lambda_q)[0].reshape(-1), f(diff_lambda_k)[0].reshape(-1)])
    lqk = np.ascontiguousarray(np.broadcast_to(lqk[None, :], (128, 256)))
    ropes = [_rope_tables(_own_pos(r)) for r in range(2)]
    dmask = _dmask()

    ncores = 8
    if "nc" not in _CACHE:
        _CACHE["nc"] = build_fused()
    shared = {"vecs0": vecs0, "win": f(gmlp_w_in)[0], "wout": f(gmlp_w_out)[0], "wgu0": f(ffn_w_gu)[0],
              "wdn0": f(ffn_w_down)[0], "wkvx": wkvx, "Rm": Rm, "wsT": wsT, "bsb": bsb, "tri": tri, "vecs1": vecs1,
              "lqk": lqk, "wqx": wqx, "wo": f(diff_w_o)[0], "wgu1": f(ffn_w_gu)[1], "wdn1": f(ffn_w_down)[1]}
    in_maps = []
    for c in range(ncores):
        b, role = c // 2, c % 2
        order = np.concatenate([_own_pos(role), _own_pos(1 - role)])
        d = dict(shared)
        d["xT"] = np.ascontiguousarray(x[b][order].T)
        d["cosT"] = np.ascontiguousarray(np.concatenate([ropes[role][0], ropes[1 - role][0]], axis=1))
        d["sinT"] = np.ascontiguousarray(np.concatenate([ropes[role][1], ropes[1 - role][1]], axis=1))
        d["dmask"] = dmask
        d["vis"] = _vis(role)
        in_maps.append(d)
    res = run_bass_kernel_spmd(_CACHE["nc"], in_maps, core_ids=list(range(ncores))).results
    out = np.empty((4, SEQ, D), np.float32)
    for c in range(ncores):
        b, role = c // 2, c % 2
        out[b][_own_pos(role)] = res[c]["outT"].T
    return out
```
